# Optimizing a Trainium2 kernel written in Bass

```python
import math
import jax, jax.numpy as jnp
from jax import lax
import numpy as np

D_MODEL = 2048
BATCH = 2
SEQ = 8192
DEPTH = 2

N_MIXERS = 2
N_MLSTM = (DEPTH + 1) // 2
N_GDN = DEPTH // 2
CHUNK = 64
EPS = 1e-6
ML_H = 4
ML_DV = D_MODEL // ML_H
ML_DQK = ML_DV // 2
GATE_SOFTCAP = 15.0
GK_H = 16
GV_H = 32
GD_K = 128
GD_V = 128
CONV_K = 4
MEM_LEN = 256
XA_H = 4
XA_DH = 128
D_FF = 4 * D_MODEL

ML_SPLITS = np.cumsum([ML_H * ML_DQK, ML_H * ML_DQK, ML_H * ML_DV, ML_H * ML_DV, ML_H]).tolist()
ML_IN = 2 * ML_H * ML_DQK + 2 * ML_H * ML_DV + 2 * ML_H
GDN_QKV = 2 * GK_H * GD_K + GV_H * GD_V
GDN_SPLITS = np.cumsum([GDN_QKV, GV_H * GD_V, GV_H]).tolist()
GDN_IN = GDN_QKV + GV_H * GD_V + 2 * GV_H

kernel_name = "hybrid_mlstm_gdn_memxattn_sandwich"


def rms_norm(x, g):
    xf = x.astype(jnp.float32)
    y = xf * lax.rsqrt(jnp.mean(xf * xf, axis=-1, keepdims=True) + EPS)
    return (y * g.astype(jnp.float32)).astype(x.dtype)


def l2_norm(x):
    return x * lax.rsqrt(jnp.sum(x * x, axis=-1, keepdims=True) + EPS)


def to_chunks(t):
    b, s, h = t.shape[:3]
    t = t.reshape((b, s // CHUNK, CHUNK, h) + t.shape[3:])
    return t.transpose((1, 0, 3, 2) + tuple(range(4, t.ndim)))


def from_chunks(t):
    nc, b, h, l, d = t.shape
    return t.transpose(1, 0, 3, 2, 4).reshape(b, nc * l, h, d)


def causal_dwconv(x, w):
    k, c = w.shape
    return lax.conv_general_dilated(x, w[:, None, :].astype(x.dtype), window_strides=(1,),
                                    padding=[(k - 1, 0)], dimension_numbers=("NWC", "WIO", "NWC"),
                                    feature_group_count=c)


def mlstm_mixer(h, w_in, b_gates, head_g, w_out):
    B, S, _ = h.shape
    p = (h @ w_in).astype(jnp.float32)
    q, k, v, o, ig, fg = jnp.split(p, ML_SPLITS, axis=-1)
    q = q.reshape(B, S, ML_H, ML_DQK) * (ML_DQK ** -0.5)
    k = k.reshape(B, S, ML_H, ML_DQK)
    v = v.reshape(B, S, ML_H, ML_DV)
    bg = b_gates.astype(jnp.float32)
    ig = GATE_SOFTCAP * jnp.tanh((ig + bg[:ML_H]) / GATE_SOFTCAP)
    fg = GATE_SOFTCAP * jnp.tanh((fg + bg[ML_H:]) / GATE_SOFTCAP)
    logf = jax.nn.log_sigmoid(fg)
    qc, kc, vc = to_chunks(q), to_chunks(k), to_chunks(v)
    igc, lfc = to_chunks(ig), to_chunks(logf)
    bcum = jnp.cumsum(lfc, axis=-1)
    causal = jnp.tril(jnp.ones((CHUNK, CHUNK), dtype=bool))
    dmat = jnp.where(causal, bcum[..., :, None] - bcum[..., None, :] + igc[..., None, :], -jnp.inf)
    dmax = dmat.max(-1)
    b_end = bcum[..., -1]
    w_end = b_end[..., None] - bcum + igc
    w_end_max = w_end.max(-1)

    def step(carry, xs):
        C, n, m = carry
        q_, k_, v_, b_, dm, dmx, be, we, wem = xs
        inter = b_ + m[..., None]
        mt = jnp.maximum(inter, dmx)
        a_inter = jnp.exp(inter - mt)
        pm = jnp.einsum('bhtd,bhsd->bhts', q_, k_) * jnp.exp(dm - mt[..., None])
        num = a_inter[..., None] * jnp.einsum('bhtd,bhde->bhte', q_, C) + jnp.einsum('bhts,bhse->bhte', pm, v_)
        den = a_inter * jnp.einsum('bhtd,bhd->bht', q_, n) + pm.sum(-1)
        out = num / jnp.maximum(jnp.abs(den), jnp.exp(-mt))[..., None]
        m_new = jnp.maximum(be + m, wem)
        a_state = jnp.exp(be + m - m_new)
        wk = k_ * jnp.exp(we - m_new[..., None])[..., None]
        C = a_state[..., None, None] * C + jnp.einsum('bhsd,bhse->bhde', wk, v_)
        n = a_state[..., None] * n + wk.sum(-2)
        return (C, n, m_new), out

    init = (jnp.zeros((B, ML_H, ML_DQK, ML_DV), jnp.float32),
            jnp.zeros((B, ML_H, ML_DQK), jnp.float32),
            jnp.zeros((B, ML_H), jnp.float32))
    _, hc = lax.scan(step, init, (qc, kc, vc, bcum, dmat, dmax, b_end, w_end, w_end_max))
    hs = rms_norm(from_chunks(hc), head_g.reshape(ML_H, ML_DV))
    y = hs.reshape(B, S, ML_H * ML_DV) * jax.nn.sigmoid(o)
    return y.astype(h.dtype) @ w_out


def gdn_mixer(h, w_in, conv_w, a_log, dt_bias, norm_g, w_out):
    B, S, _ = h.shape
    p = h @ w_in
    qkv, z, bt, a = jnp.split(p, GDN_SPLITS, axis=-1)
    qkv = jax.nn.silu(causal_dwconv(qkv, conv_w)).astype(jnp.float32)
    q, k, v = jnp.split(qkv, [GK_H * GD_K, 2 * GK_H * GD_K], axis=-1)
    q = l2_norm(q.reshape(B, S, GK_H, GD_K)) * (GD_K ** -0.5)
    k = l2_norm(k.reshape(B, S, GK_H, GD_K))
    q = jnp.repeat(q, GV_H // GK_H, axis=2)
    k = jnp.repeat(k, GV_H // GK_H, axis=2)
    v = v.reshape(B, S, GV_H, GD_V)
    beta = jax.nn.sigmoid(bt.astype(jnp.float32))
    g = -jnp.exp(a_log.astype(jnp.float32)) * jax.nn.softplus(a.astype(jnp.float32) + dt_bias.astype(jnp.float32))
    qc, kc, vc = to_chunks(q), to_chunks(k), to_chunks(v)
    betac, gam = to_chunks(beta), jnp.cumsum(to_chunks(g), axis=-1)
    diff = gam[..., :, None] - gam[..., None, :]
    strict = jnp.tril(jnp.ones((CHUNK, CHUNK), dtype=bool), -1)
    causal = jnp.tril(jnp.ones((CHUNK, CHUNK), dtype=bool))
    kb = kc * betac[..., None]
    vb = vc * betac[..., None]
    A = jnp.einsum('...id,...jd->...ij', kb, kc) * jnp.exp(jnp.where(strict, diff, -jnp.inf))
    U = lax.linalg.triangular_solve(A, vb, left_side=True, lower=True, unit_diagonal=True)
    W = lax.linalg.triangular_solve(A, kb * jnp.exp(gam)[..., None], left_side=True, lower=True, unit_diagonal=True)
    Aqk = jnp.einsum('...id,...jd->...ij', qc, kc) * jnp.exp(jnp.where(causal, diff, -jnp.inf))
    qg = qc * jnp.exp(gam)[..., None]
    kd = kc * jnp.exp(gam[..., -1:] - gam)[..., None]
    dl = jnp.exp(gam[..., -1])

    def step(Sst, xs):
        u_, w_, qg_, kd_, aqk_, dl_ = xs
        vn = u_ - jnp.einsum('bhld,bhde->bhle', w_, Sst)
        o = jnp.einsum('bhld,bhde->bhle', qg_, Sst) + jnp.einsum('bhls,bhse->bhle', aqk_, vn)
        Sst = dl_[..., None, None] * Sst + jnp.einsum('bhld,bhle->bhde', kd_, vn)
        return Sst, o

    S0 = jnp.zeros((B, GV_H, GD_K, GD_V), jnp.float32)
    _, oc = lax.scan(step, S0, (U, W, qg, kd, Aqk, dl))
    o = rms_norm(from_chunks(oc), norm_g) * jax.nn.silu(z.astype(jnp.float32).reshape(B, S, GV_H, GD_V))
    return o.reshape(B, S, GV_H * GD_V).astype(h.dtype) @ w_out


def mem_cross_attn(h, k_m, v_m, w_q, w_o):
    B, S, _ = h.shape
    q = (h @ w_q).reshape(B, S, XA_H, XA_DH)
    s = jnp.einsum('bshd,bmhd->bhsm', q, k_m).astype(jnp.float32) * (XA_DH ** -0.5)
    pr = jax.nn.softmax(s, axis=-1).astype(h.dtype)
    o = jnp.einsum('bhsm,bmhd->bshd', pr, v_m).reshape(B, S, XA_H * XA_DH)
    return o @ w_o


def sq_relu_mlp(h, w_up, w_down):
    u = jax.nn.relu(h @ w_up)
    return (u * u) @ w_down


def setup_inputs(seed: int = 0) -> dict:
    key = jax.random.key(seed)
    ks = jax.random.split(key, 24)
    f32 = jnp.float32

    def nrm(k, shape, fan_in):
        return jax.random.normal(k, shape, f32) * (fan_in ** -0.5)

    def gain(k, shape):
        return 1.0 + 0.02 * jax.random.normal(k, shape, f32)

    dt = jnp.exp(jax.random.uniform(ks[20], (N_GDN, GV_H), f32) * (math.log(0.1) - math.log(0.001)) + math.log(0.001))
    return {
        "x": jax.random.normal(ks[0], (BATCH, SEQ, D_MODEL), f32),
        "mem": jax.random.normal(ks[1], (BATCH, MEM_LEN, D_MODEL), f32),
        "norm_g": gain(ks[2], (DEPTH, 6, D_MODEL)),
        "mem_norm_g": gain(ks[3], (D_MODEL,)),
        "w_mem_kv": nrm(ks[4], (D_MODEL, 2 * XA_H * XA_DH), D_MODEL),
        "w_xq": nrm(ks[5], (DEPTH, D_MODEL, XA_H * XA_DH), D_MODEL),
        "w_xo": nrm(ks[6], (DEPTH, XA_H * XA_DH, D_MODEL), XA_H * XA_DH),
        "w_up": nrm(ks[7], (DEPTH, D_MODEL, D_FF), D_MODEL),
        "w_down": nrm(ks[8], (DEPTH, D_FF, D_MODEL), D_FF),
        "mlstm_w_in": nrm(ks[9], (N_MLSTM, D_MODEL, ML_IN), D_MODEL),
        "mlstm_b_gates": jnp.concatenate([0.1 * jax.random.normal(ks[10], (N_MLSTM, ML_H), f32),
                                          3.0 + 3.0 * jax.random.uniform(ks[11], (N_MLSTM, ML_H), f32)], axis=-1),
        "mlstm_head_g": gain(ks[12], (N_MLSTM, ML_H * ML_DV)),
        "mlstm_w_out": nrm(ks[13], (N_MLSTM, ML_H * ML_DV, D_MODEL), ML_H * ML_DV),
        "gdn_w_in": nrm(ks[14], (N_GDN, D_MODEL, GDN_IN), D_MODEL),
        "gdn_conv_w": nrm(ks[15], (N_GDN, CONV_K, GDN_QKV), CONV_K),
        "gdn_a_log": jnp.log(jax.random.uniform(ks[16], (N_GDN, GV_H), f32, 1.0, 16.0)),
        "gdn_dt_bias": dt + jnp.log(-jnp.expm1(-dt)),
        "gdn_norm_g": gain(ks[17], (N_GDN, GD_V)),
        "gdn_w_out": nrm(ks[18], (N_GDN, GV_H * GD_V, D_MODEL), GV_H * GD_V),
    }


def reference(x, mem, norm_g, mem_norm_g, w_mem_kv, w_xq, w_xo, w_up, w_down,
              mlstm_w_in, mlstm_b_gates, mlstm_head_g, mlstm_w_out,
              gdn_w_in, gdn_conv_w, gdn_a_log, gdn_dt_bias, gdn_norm_g, gdn_w_out):
    B = mem.shape[0]
    kv = (rms_norm(mem, mem_norm_g) @ w_mem_kv).reshape(B, MEM_LEN, 2, XA_H, XA_DH)
    k_m, v_m = kv[:, :, 0], kv[:, :, 1]
    for i in range(DEPTH):
        g = norm_g[i]
        j = i // N_MIXERS
        h = rms_norm(x, g[0])
        if i % N_MIXERS == 0:
            mix = mlstm_mixer(h, mlstm_w_in[j], mlstm_b_gates[j], mlstm_head_g[j], mlstm_w_out[j])
        else:
            mix = gdn_mixer(h, gdn_w_in[j], gdn_conv_w[j], gdn_a_log[j], gdn_dt_bias[j], gdn_norm_g[j], gdn_w_out[j])
        x = x + rms_norm(mix, g[1])
        h = rms_norm(x, g[2])
        x = x + rms_norm(mem_cross_attn(h, k_m, v_m, w_xq[i], w_xo[i]), g[3])
        h = rms_norm(x, g[4])
        x = x + rms_norm(sq_relu_mlp(h, w_up[i], w_down[i]), g[5])
    return x
```

```python
import ml_dtypes
import contextlib
import numpy as np
import concourse.bass as bass
import concourse.mybir as mybir
from concourse.bass_utils import run_bass_kernel_spmd

F32 = mybir.dt.float32
BF16 = mybir.dt.bfloat16
AF = mybir.ActivationFunctionType
ALU = mybir.AluOpType
AX = mybir.AxisListType


class Prog:
    ENGS = ("pe", "act", "dve", "pool", "sp")

    def __init__(self, nc):
        self.nc = nc
        self.es = contextlib.ExitStack()
        self.ops = {e: [] for e in self.ENGS}
        self.cnt = {e: 0 for e in self.ENGS}
        self.sem = {}
        for e in ("pe", "act", "dve", "pool"):
            self.sem[e] = self.es.enter_context(nc.semaphore("s_" + e))
        self.dsem = {}
        self.seen = {e: {} for e in self.ENGS}
        self.lastw = {}
        self.readers = {}
        self.nbuf = 0

    def sb(self, name, shape, dt):
        return self.es.enter_context(self.nc.sbuf_tensor(name, list(shape), dt))

    def ps(self, name, shape, dt):
        return self.es.enter_context(self.nc.psum_tensor(name, list(shape), dt))

    def _deps(self, eng, own_sem, r, w, x=()):
        deps = {}

        def add(ev, same_ok):
            if ev is None:
                return
            s, v = ev
            if s is own_sem and same_ok:
                return
            k = id(s)
            if k not in deps or deps[k][1] < v:
                deps[k] = (s, v)

        for k in r:
            add(self.lastw.get(k), False)
        for k in x:
            add(self.lastw.get(k), False)
            for ev in self.readers.get(k, ()):
                add(ev, True)
        for k in w:
            add(self.lastw.get(k), True)
            for ev in self.readers.get(k, ()):
                add(ev, True)
        out = []
        seen = self.seen[eng]
        for k, (s, v) in deps.items():
            if seen.get(k, 0) >= v:
                continue
            seen[k] = v
            out.append((s, v))
        return out

    def _commit(self, ev, r, w):
        for k in r:
            self.readers.setdefault(k, []).append(ev)
        for k in w:
            self.lastw[k] = ev
            self.readers[k] = []

    def op(self, eng, fn, r=(), w=(), sig=True, x=()):
        own = self.sem[eng]
        waits = self._deps(eng, own, r, w, x)
        r = list(r) + list(x)
        if eng == "pe":
            waits = [(s, v) for (s, v) in waits if s is not own]
        ev = (own, self.cnt[eng] + 1)
        if sig:
            self.cnt[eng] += 1
        self.ops[eng].append((fn, waits, own if sig else None, 1))
        self._commit(ev, r, w)
        if sig and self.cnt[eng] >= 30000:
            self.sem[eng] = self.es.enter_context(self.nc.semaphore("s_%s_%d" % (eng, len(self.ops[eng]))))
            self.cnt[eng] = 0

    def dma(self, q, out, in_, r=(), w=(), key=None, **kw):
        if key is None:
            key = w[0] if w else r[0]
        if key not in self.dsem:
            self.dsem[key] = [self.es.enter_context(self.nc.semaphore("d%d" % len(self.dsem))), 0]
        ds = self.dsem[key]
        waits = self._deps(q, None, r, w)
        ds[1] += 16
        ev = (ds[0], ds[1])
        self.ops[q].append((lambda e: e.dma_start(out=out, in_=in_, **kw), waits, ds[0], 16))
        self._commit(ev, r, w)
        return ev

    def wait_all(self, eng, keys):
        waits = self._deps(eng, None, keys, ())
        self.ops[eng].append((None, waits, None, 0))

    def emit(self):
        nc = self.nc
        engmap = {"pe": "tensor", "act": "scalar", "dve": "vector", "pool": "gpsimd", "sp": "sync"}
        with nc.Block() as block:
            for e in self.ENGS:
                ops = self.ops[e]
                if not ops:
                    continue

                def body(eh, ops=ops):
                    for fn, waits, s, inc in ops:
                        for (ws, wv) in waits:
                            eh.wait_ge(ws, wv)
                        if fn is not None:
                            ins = fn(eh)
                            if s is not None:
                                ins.then_inc(s, inc)

                getattr(block, engmap[e])(body)

    def close(self):
        self.es.close()

import ml_dtypes

EPS = 1e-6
TB = 512


class TL:
    def __init__(self, P, nc):
        self.P = P
        self.nc = nc
        P_ = P
        self.lin_ps = [P_.ps("lps%d" % i, [128, 512], F32) for i in range(6)]
        self.st_ps = P_.ps("stps", [128, 512], F32)
        self.at_ps = P_.ps("atps", [128, 512], F32)
        self.NW = 3
        self.wring = [P_.sb("wr%d" % i, [128, 8, 256], BF16) for i in range(self.NW)]
        self.wcnt = 0
        self.setc = 0
        self.ones = P_.sb("ones", [128, 128], BF16)
        P_.op("dve", lambda e: e.memset(self.ones[:], 1.0), w=["ones"])
        self.ident = P_.sb("ident", [128, 128], BF16)
        self.sq = [P_.sb("sq%d" % i, [128, 512], BF16) for i in range(2)]
        self.sqc = 0
        self.rstd = P_.sb("rstd", [128, 512], F32)
        self.tmp = [P_.sb("tmp%d" % i, [128, 512], F32) for i in range(2)]
        self.tmpc = 0

    def linear(self, W, K, M, in_sb, in_key, evac, ntok=TB):
        P = self.P
        KT = K // 128
        kgs = [(k0, min(8, KT - k0)) for k0 in range(0, KT, 8)]
        for mg in range(M // 256):
            st = self.setc % 3
            self.setc += 1
            pk = ["lps%d" % (st * 2), "lps%d" % (st * 2 + 1)]
            for (k0, nk) in kgs:
                b = self.wcnt % self.NW
                self.wcnt += 1
                wk = "wr%d" % b
                src = W[k0 * 128:(k0 + nk) * 128, mg * 256:(mg + 1) * 256].rearrange("(k p) m -> p k m", p=128)
                P.dma("pool", self.wring[b][:, 0:nk, :], src, w=[wk])
                for k in range(nk):
                    for j in range(2):
                        kk = k0 + k
                        P.op("pe", lambda e, b=b, k=k, j=j, kk=kk, st=st: e.matmul(
                            self.lin_ps[st * 2 + j][:, 0:ntok], lhsT=self.wring[b][:, k, j * 128:(j + 1) * 128],
                            rhs=in_sb[:, kk, 0:ntok], start=(kk == 0), stop=(kk == KT - 1)),
                            r=[wk, in_key], w=[pk[j]], sig=(kk == KT - 1 or k == nk - 1))
            for j in range(2):
                evac(mg * 2 + j, self.lin_ps[st * 2 + j], pk[j])

    def sq_acc(self, src_ap, src_keys, idx, n, ntok=TB, psum=False):
        P = self.P
        b = self.sqc % 2
        self.sqc += 1
        sq = self.sq[b]
        P.op("act", lambda e: e.activation(out=sq[:, 0:ntok], in_=src_ap, func=AF.Square), r=([] if psum else src_keys), x=(src_keys if psum else []), w=["sq%d" % b])
        P.op("pe", lambda e: e.matmul(self.st_ps[:, 0:ntok], lhsT=self.ones[:], rhs=sq[:, 0:ntok], start=(idx == 0), stop=(idx == n - 1)),
             r=["sq%d" % b, "ones"], w=["stps"], sig=True)

    def make_rstd(self, nfeat, ntok=TB):
        P = self.P
        P.op("dve", lambda e: e.tensor_scalar(out=self.rstd[:, 0:ntok], in0=self.st_ps[:, 0:ntok], scalar1=1.0 / nfeat, scalar2=EPS,
                                              op0=ALU.mult, op1=ALU.add), x=["stps"], w=["rstd"])
        P.op("act", lambda e: e.activation(out=self.rstd[:, 0:ntok], in_=self.rstd[:, 0:ntok], func=AF.Sqrt), r=["rstd"], w=["rstd"])
        P.op("dve", lambda e: e.reciprocal(out=self.rstd[:, 0:ntok], in_=self.rstd[:, 0:ntok]), r=["rstd"], w=["rstd"])

    def norm_to_bf16(self, x_sb, xkey, g_sb, gi, h_sb, hkey, ntok=TB, nt=16):
        P = self.P
        for dt in range(nt):
            self.sq_acc(x_sb[:, dt, 0:ntok], [xkey], dt, nt, ntok)
        self.make_rstd(nt * 128, ntok)
        for dt in range(nt):
            P.op("dve", lambda e, dt=dt: e.scalar_tensor_tensor(out=h_sb[:, dt, 0:ntok], in0=x_sb[:, dt, 0:ntok],
                 scalar=g_sb[:, gi, dt:dt + 1], in1=self.rstd[:, 0:ntok], op0=ALU.mult, op1=ALU.mult),
                 r=[xkey, "rstd", "g"], w=[hkey])

    def normres(self, d_sb, dkey, g_sb, gi, x_sb, xkey, ntok=TB):
        P = self.P
        self.make_rstd(2048, ntok)
        for dt in range(16):
            b = self.tmpc % 2
            self.tmpc += 1
            t = self.tmp[b]
            P.op("dve", lambda e, dt=dt, t=t: e.tensor_tensor(out=t[:, 0:ntok], in0=d_sb[:, dt, 0:ntok],
                 in1=self.rstd[:, 0:ntok], op=ALU.mult), r=[dkey, "rstd"], w=["tmp%d" % b])
            P.op("dve", lambda e, dt=dt, t=t: e.scalar_tensor_tensor(out=x_sb[:, dt, 0:ntok], in0=t[:, 0:ntok],
                 scalar=g_sb[:, gi, dt:dt + 1], in1=x_sb[:, dt, 0:ntok], op0=ALU.mult, op1=ALU.add),
                 r=["tmp%d" % b, xkey, "g"], w=[xkey])

    def evac_d(self, d_sb, dkey, ntok=TB):
        P = self.P

        def ev(mt, ps, pk):
            P.op("dve", lambda e: e.tensor_copy(out=d_sb[:, mt, 0:ntok], in_=ps[:, 0:ntok]), x=[pk], w=[dkey])
            import os
            if os.environ.get("NOSQ") != "1":
                self.sq_acc(ps[:, 0:ntok], [pk], mt, 16, ntok, psum=True)
        return ev


def build_tl(NT, KY, last, dbg=False, stage=99):
    nc = bass.Bass("TRN2", target_bir_lowering=False)
    dr = lambda n, s, d, k="ExternalInput": nc.dram_tensor(n, list(s), d, kind=k).ap()
    yT = dr("yT", [KY, NT], BF16)
    xT = dr("xT", [2048, NT], F32)
    memT = dr("memT", [2048, 256], F32)
    gT = dr("gT", [128, 8, 16], F32)
    w_out = dr("w_out", [KY, 2048], F32)
    w_xq = dr("w_xq", [2048, 512], F32)
    w_xo = dr("w_xo", [512, 2048], F32)
    w_up = dr("w_up", [2048, 8192], F32)
    w_down = dr("w_down", [8192, 2048], F32)
    w_kv = dr("w_kv", [2048, 1024], F32)
    identd = dr("identd", [128, 128], BF16)
    xo = dr("xo", [2048, NT], F32, "ExternalOutput")
    ho = None if last else dr("ho", [2048, NT], BF16, "ExternalOutput")
    P = Prog(nc)
    T = TL(P, nc)
    KYT = KY // 128
    g_sb = P.sb("g", [128, 8, 16], F32)
    P.dma("sp", g_sb[:], gT, w=["g"])
    P.dma("sp", T.ident[:], identd, w=["ident"])
    x_sb = P.sb("x", [128, 16, TB], F32)
    d_sb = P.sb("d", [128, 16, TB], F32)
    h_sb = P.sb("h", [128, 16, TB], BF16)
    u_sb = P.sb("u", [128, 64, TB], BF16)
    if KY > 2048:
        y_sb, ykey = u_sb, "u"
    else:
        y_sb, ykey = P.sb("y", [128, KYT, TB], BF16), "y"
    q_sb = P.sb("q", [128, 4, TB], BF16)
    o_sb = P.sb("o", [128, 4, TB], BF16)
    kmT = P.sb("kmT", [128, 4, 256], BF16)
    vm = P.sb("vm", [128, 2, 512], BF16)
    P.dma("sp", x_sb[:, :, 0:256], memT.rearrange("(t p) m -> p t m", p=128), w=["x"])
    if stage >= 0.2:
        T.norm_to_bf16(x_sb, "x", g_sb, 6, h_sb, "h", ntok=256)

    def ev_k(mt, ps, pk):
        P.op("act", lambda e: e.activation(out=kmT[:, mt, :], in_=ps[:, 0:256], func=AF.Copy), x=[pk], w=["kmT"])
    if stage >= 0.3:
        T.linear(w_kv[:, 0:512], 2048, 512, h_sb, "h", ev_k, ntok=256)
    vT = P.sb("vT", [128, 4, 256], BF16)

    def ev_v(mt, ps, pk):
        P.op("act", lambda e: e.activation(out=vT[:, mt, :], in_=ps[:, 0:256], func=AF.Copy), x=[pk], w=["vT"])
    if stage >= 0.4:
        T.linear(w_kv[:, 512:1024], 2048, 512, h_sb, "h", ev_v, ntok=256)
    tp_ps = T.at_ps[:, 384:512].bitcast(BF16)
    for hd in range(4 if stage >= 1 else 0):
        for mt in range(2):
            P.op("pe", lambda e, hd=hd, mt=mt: e.transpose(out=tp_ps[:, 0:128], in_=vT[:, hd, mt * 128:(mt + 1) * 128], identity=T.ident[:]),
                 r=["vT", "ident"], w=["atps"])
            P.op("act", lambda e, hd=hd, mt=mt: e.activation(out=vm[:, mt, hd * 128:(hd + 1) * 128], in_=tp_ps[:, 0:128], func=AF.Copy),
                 x=["atps"], w=["vm"])
    SC = 128 ** -0.5
    mx = P.sb("mx", [128, 1], F32)
    rs = P.sb("rs", [128, 1], F32)
    p_sb = P.sb("p", [128, 256], F32)
    pn_sb = P.sb("pn", [128, 256], BF16)
    pT_sb = P.sb("pT", [128, 2, 128], BF16)

    for blk in range(NT // TB):
        c0 = blk * TB
        P.dma("sp", y_sb[:, 0:KYT, :], yT[:, c0:c0 + TB].rearrange("(t p) n -> p t n", p=128), w=[ykey])
        P.dma("sp", x_sb[:], xT[:, c0:c0 + TB].rearrange("(t p) n -> p t n", p=128), w=["x"])
        if stage >= 2:
            T.linear(w_out, KY, 2048, y_sb, ykey, T.evac_d(d_sb, "d"))
            import os
            if os.environ.get("NONR") != "1":
                T.normres(d_sb, "d", g_sb, 0, x_sb, "x")
        if stage < 3:
            P.dma("sp", xo[:, c0:c0 + TB].rearrange("(t p) n -> p t n", p=128), x_sb[:], r=["x"], w=["xo"])
            P.dma("sp", ho[:, c0:c0 + 256].rearrange("(t p) n -> p t n", p=128), h_sb[:, :, 0:256], r=["h"], w=["ho"])
            continue
        T.norm_to_bf16(x_sb, "x", g_sb, 1, h_sb, "h")

        def ev_q(mt, ps, pk):
            P.op("act", lambda e: e.activation(out=q_sb[:, mt, :], in_=ps[:], func=AF.Copy), x=[pk], w=["q"])
        T.linear(w_xq, 2048, 512, h_sb, "h", ev_q)
        for tt in range(4):
            for hd in range(4):
                P.op("pe", lambda e, tt=tt, hd=hd: e.matmul(T.at_ps[:, 0:256], lhsT=q_sb[:, hd, tt * 128:(tt + 1) * 128], rhs=kmT[:, hd, :],
                     start=True, stop=True), r=["q", "kmT"], w=["atps"])
                P.op("dve", lambda e: e.reduce_max(out=mx[:], in_=T.at_ps[:, 0:256], axis=AX.X), x=["atps"], w=["mx"])
                P.op("dve", lambda e: e.tensor_scalar(out=mx[:], in0=mx[:], scalar1=-SC, scalar2=None, op0=ALU.mult), r=["mx"], w=["mx"])
                P.op("act", lambda e: e.activation(out=p_sb[:], in_=T.at_ps[:, 0:256], func=AF.Exp, bias=mx[:], scale=SC, accum_out=rs[:]),
                     r=["mx"], x=["atps"], w=["p", "rs"])
                P.op("dve", lambda e: e.reciprocal(out=rs[:], in_=rs[:]), r=["rs"], w=["rs"])
                P.op("dve", lambda e: e.tensor_scalar(out=pn_sb[:], in0=p_sb[:], scalar1=rs[:, 0:1], scalar2=None, op0=ALU.mult),
                     r=["p", "rs"], w=["pn"])
                for mt in range(2):
                    P.op("pe", lambda e, mt=mt: e.transpose(out=tp_ps[:, mt * 128:(mt + 1) * 128], in_=pn_sb[:, mt * 128:(mt + 1) * 128], identity=T.ident[:]),
                         r=["pn", "ident"], w=["atps"])
                P.op("act", lambda e: e.activation(out=pT_sb[:].rearrange("p a b -> p (a b)"), in_=tp_ps[:, 0:256], func=AF.Copy), x=["atps"], w=["pT"])
                for mt in range(2):
                    P.op("pe", lambda e, mt=mt, hd=hd: e.matmul(T.at_ps[:, 256:384], lhsT=vm[:, mt, hd * 128:(hd + 1) * 128], rhs=pT_sb[:, mt, :],
                         start=(mt == 0), stop=(mt == 1)), r=["vm", "pT"], w=["atps"], sig=(mt == 1))
                P.op("act", lambda e, tt=tt, hd=hd: e.activation(out=o_sb[:, hd, tt * 128:(tt + 1) * 128], in_=T.at_ps[:, 256:384], func=AF.Copy),
                     x=["atps"], w=["o"])
        T.linear(w_xo, 512, 2048, o_sb, "o", T.evac_d(d_sb, "d"))
        T.normres(d_sb, "d", g_sb, 2, x_sb, "x")
        T.norm_to_bf16(x_sb, "x", g_sb, 3, h_sb, "h")

        def ev_u(mt, ps, pk):
            b = T.tmpc % 2
            T.tmpc += 1
            t = T.tmp[b]
            P.op("act", lambda e: e.activation(out=t[:], in_=ps[:], func=AF.Relu), x=[pk], w=["tmp%d" % b])
            P.op("dve", lambda e: e.tensor_tensor(out=u_sb[:, mt, :], in0=t[:], in1=t[:], op=ALU.mult), r=["tmp%d" % b], w=["u"])
        T.linear(w_up, 2048, 8192, h_sb, "h", ev_u)
        T.linear(w_down, 8192, 2048, u_sb, "u", T.evac_d(d_sb, "d"))
        T.normres(d_sb, "d", g_sb, 4, x_sb, "x")
        P.dma("sp", xo[:, c0:c0 + TB].rearrange("(t p) n -> p t n", p=128), x_sb[:], r=["x"], w=["xo"])
        if not last:
            T.norm_to_bf16(x_sb, "x", g_sb, 5, h_sb, "h")
            P.dma("sp", ho[:, c0:c0 + TB].rearrange("(t p) n -> p t n", p=128), h_sb[:], r=["h"], w=["ho"])
    P.wait_all("sp", ["xo"] + ([] if last else ["ho"]))
    P.emit()
    P.close()
    return nc


def build_a0(NT):
    nc = bass.Bass("TRN2", target_bir_lowering=False)
    dr = lambda n, s, d, k="ExternalInput": nc.dram_tensor(n, list(s), d, kind=k).ap()
    xT = dr("xT", [2048, NT], F32)
    gT = dr("gT", [128, 8, 16], F32)
    ho = dr("ho", [2048, NT], BF16, "ExternalOutput")
    P = Prog(nc)
    T = TL(P, nc)
    g_sb = P.sb("g", [128, 8, 16], F32)
    P.dma("sp", g_sb[:], gT, w=["g"])
    x_sb = P.sb("x", [128, 16, TB], F32)
    h_sb = P.sb("h", [128, 16, TB], BF16)
    for blk in range(NT // TB):
        c0 = blk * TB
        P.dma("sp", x_sb[:], xT[:, c0:c0 + TB].rearrange("(t p) n -> p t n", p=128), w=["x"])
        T.norm_to_bf16(x_sb, "x", g_sb, 0, h_sb, "h")
        P.dma("sp", ho[:, c0:c0 + TB].rearrange("(t p) n -> p t n", p=128), h_sb[:], r=["h"], w=["ho"])
    P.wait_all("sp", ["ho"])
    P.emit()
    P.close()
    return nc


def build_ml(S):
    nc = bass.Bass("TRN2", target_bir_lowering=False)
    dr = lambda n, s, d, k="ExternalInput": nc.dram_tensor(n, list(s), d, kind=k).ap()
    hT = dr("hT", [2048, S], BF16)
    w_fm = dr("w_fm", [2048, 512], F32)
    w_tm = dr("w_tm", [2048, 1282], F32)
    bgd = dr("bg", [128, 2], F32)
    hgd = dr("hg", [128, 512], F32)
    trid = dr("tri", [128, 128], F32)
    nmd = dr("negmask", [128, 128], F32)
    y = dr("y", [S, 512], BF16, "ExternalOutput")
    P = Prog(nc)
    wfm = P.sb("wfm", [128, 16, 512], BF16)
    wtm = P.sb("wtm", [128, 16, 1282], BF16)
    for k0 in range(0, 16, 4):
        P.dma("pool", wfm[:, k0:k0 + 4, :], w_fm[k0 * 128:(k0 + 4) * 128, :].rearrange("(k p) m -> p k m", p=128), w=["wfm"])
        P.dma("pool", wtm[:, k0:k0 + 4, :], w_tm[k0 * 128:(k0 + 4) * 128, :].rearrange("(k p) m -> p k m", p=128), w=["wtm"])
    b15 = P.sb("b15", [128, 2], F32)
    hg = P.sb("hg_sb", [128, 512], F32)
    tri = P.sb("tri_sb", [128, 128], F32)
    nm = P.sb("nm", [128, 128], F32)
    onesf = P.sb("onesf", [128, 128], F32)
    P.dma("sp", b15[:], bgd, w=["b15"])
    P.dma("sp", hg[:], hgd, w=["hg"])
    P.dma("sp", tri[:], trid, w=["tri"])
    P.dma("sp", nm[:], nmd, w=["nm"])
    P.op("dve", lambda e: e.memset(onesf[:], 1.0), w=["onesf"])
    P.op("dve", lambda e: e.tensor_scalar(out=b15[:], in0=b15[:], scalar1=1.0 / 15.0, scalar2=None, op0=ALU.mult), r=["b15"], w=["b15"])
    h_sb = [P.sb("h%d" % i, [128, 16, 512], BF16) for i in range(2)]
    qT = P.sb("qT", [128, 2, 512], BF16)
    kT = P.sb("kT", [128, 2, 512], BF16)
    vaug = P.sb("vaug", [128, 4, 513], BF16)
    gsig = P.sb("gsig", [128, 4, 512], F32)
    ktok = P.sb("ktok", [128, 4, 256], BF16)
    graw = P.sb("graw", [128, 4, 2], F32)
    Cf = P.sb("Cf", [128, 2, 513], F32)
    Cb = P.sb("Cb", [128, 2, 513], BF16)
    P.op("dve", lambda e: e.memset(Cf[:], 0.0), w=["Cf"])
    P.op("dve", lambda e: e.memset(Cb[:], 0.0), w=["Cb"])
    P.op("dve", lambda e: e.memset(vaug[:], 1.0), w=["vaug"])
    pf = P.ps("pf", [128, 512], F32)
    pt = P.ps("pt", [128, 512], F32)
    misc = P.ps("misc", [128, 512], F32)
    outp = P.ps("outp", [128, 512], F32)
    dC = [P.ps("dC%d" % j, [128, 512], F32) for j in range(2)]
    sm = lambda n, c=1: P.sb(n, [128, c], F32)
    th, ee, ll, cs, bend, wexp, ast, den, rr, ss, rstd = [sm(n, 2 if n in ("th", "ee", "ll") else 1) for n in
                                                          ("th", "ee", "ll", "cs", "bend", "wexp", "ast", "den", "rr", "ss", "rstd")]
    LF = P.sb("LF", [128, 128], F32)
    arg = P.sb("arg", [128, 128], F32)
    DT = P.sb("DT", [128, 128], F32)
    Arow = P.sb("Arow", [128, 128], F32)
    pmT = P.sb("pmT", [128, 128], BF16)
    qs = P.sb("qs", [128, 2, 128], BF16)
    hcs = P.sb("hcs", [128, 512], F32)
    junk = P.sb("junk", [128, 512], F32)
    ysb = P.sb("ysb", [128, 512], BF16)
    vw = P.sb("vw", [128, 513], BF16)
    sgt = P.sb("sgt", [128, 512], F32)

    for blk in range(S // 512):
        hb = h_sb[blk % 2]
        hk = "h%d" % (blk % 2)
        c0 = blk * 512
        P.dma("sp", hb[:], hT[:, c0:c0 + 512].rearrange("(t p) n -> p t n", p=128), w=[hk])
        for mt in range(4):
            for kt in range(16):
                P.op("pe", lambda e, mt=mt, kt=kt, hb=hb: e.matmul(pf[:], lhsT=wfm[:, kt, mt * 128:(mt + 1) * 128], rhs=hb[:, kt, :],
                     start=(kt == 0), stop=(kt == 15)), r=["wfm", hk], w=["pf"], sig=(kt == 15))
            if mt < 2:
                P.op("act", lambda e, mt=mt: e.activation(out=qT[:, mt, :], in_=pf[:], func=AF.Copy, scale=1.0 / 16.0), x=["pf"], w=["qT"])
            else:
                P.op("act", lambda e, mt=mt: e.activation(out=kT[:, mt - 2, :], in_=pf[:], func=AF.Copy), x=["pf"], w=["kT"])
        for tt in range(4):
            for (g0, gn) in ((0, 512), (512, 512), (1024, 258)):
                for kt in range(16):
                    P.op("pe", lambda e, tt=tt, kt=kt, g0=g0, gn=gn, hb=hb: e.matmul(pt[:, 0:gn], lhsT=hb[:, kt, tt * 128:(tt + 1) * 128],
                         rhs=wtm[:, kt, g0:g0 + gn], start=(kt == 0), stop=(kt == 15)), r=["wtm", hk], w=["pt"], sig=(kt == 15))
                if g0 == 0:
                    P.op("act", lambda e, tt=tt: e.activation(out=vaug[:, tt, 0:512], in_=pt[:], func=AF.Copy), x=["pt"], w=["vaug"])
                elif g0 == 512:
                    P.op("act", lambda e, tt=tt: e.activation(out=sgt[:], in_=pt[:], func=AF.Sigmoid), x=["pt"], w=["sgt"])
                    P.op("dve", lambda e, tt=tt: e.tensor_tensor(out=gsig[:, tt, :], in0=sgt[:], in1=hg[:], op=ALU.mult), r=["sgt", "hg"], w=["gsig"])
                else:
                    P.op("act", lambda e, tt=tt: e.activation(out=ktok[:, tt, :], in_=pt[:, 0:256], func=AF.Copy), x=["pt"], w=["ktok"])
                    P.op("dve", lambda e, tt=tt: e.tensor_copy(out=graw[:, tt, :], in_=pt[:, 256:258]), x=["pt"], w=["graw"])
        for tt in range(4):
            cc = slice(tt * 128, (tt + 1) * 128)
            for j in range(2):
                P.op("act", lambda e, j=j, tt=tt: e.activation(out=th[:, j:j + 1], in_=graw[:, tt, j:j + 1], func=AF.Tanh, bias=b15[:, j:j + 1], scale=1.0 / 15.0),
                     r=["graw", "b15"], w=["th"])
            P.op("act", lambda e: e.activation(out=ee[:, 0:1], in_=th[:, 1:2], func=AF.Exp, scale=-15.0), r=["th"], w=["ee"])
            P.op("act", lambda e: e.activation(out=ll[:, 0:1], in_=ee[:, 0:1], func=AF.Ln, bias=1.0), r=["ee"], w=["ll"])
            P.op("dve", lambda e: e.tensor_scalar(out=ll[:, 1:2], in0=ll[:, 0:1], scalar1=-1.0, scalar2=None, op0=ALU.mult), r=["ll"], w=["ll"])
            P.op("dve", lambda e: e.tensor_scalar(out=LF[:], in0=onesf[:], scalar1=ll[:, 1:2], scalar2=None, op0=ALU.mult), r=["ll", "onesf"], w=["LF"])
            P.op("pe", lambda e: e.matmul(misc[:, 0:128], lhsT=LF[:], rhs=tri[:], start=True, stop=True), r=["LF", "tri"], w=["misc"])
            P.op("pe", lambda e: e.matmul(misc[:, 257:258], lhsT=tri[:], rhs=ll[:, 1:2], start=True, stop=True), r=["ll", "tri"], w=["misc"])
            for j in range(2):
                P.op("pe", lambda e, j=j, cc=cc: e.matmul(misc[:, 128:256], lhsT=kT[:, j, cc], rhs=qT[:, j, cc], start=(j == 0), stop=(j == 1)),
                     r=["kT", "qT"], w=["misc"], sig=(j == 1))
            P.op("dve", lambda e: e.scalar_tensor_tensor(out=cs[:], in0=th[:, 0:1], scalar=15.0, in1=misc[:, 257:258], op0=ALU.mult, op1=ALU.subtract),
                 r=["th"], x=["misc"], w=["cs"])
            P.op("dve", lambda e: e.scalar_tensor_tensor(out=arg[:], in0=misc[:, 0:128], scalar=cs[:, 0:1], in1=nm[:], op0=ALU.add, op1=ALU.add),
                 r=["cs", "nm"], x=["misc"], w=["arg"])
            P.op("dve", lambda e: e.tensor_copy(out=bend[:], in_=misc[:, 127:128]), x=["misc"], w=["bend"])
            P.op("act", lambda e: e.activation(out=Arow[:], in_=misc[:, 0:128], func=AF.Exp), x=["misc"], w=["Arow"])
            P.op("act", lambda e: e.activation(out=DT[:], in_=arg[:], func=AF.Exp), r=["arg"], w=["DT"])
            P.op("act", lambda e: e.activation(out=wexp[:], in_=cs[:], func=AF.Exp, bias=bend[:, 0:1]), r=["cs", "bend"], w=["wexp"])
            P.op("act", lambda e: e.activation(out=ast[:], in_=bend[:], func=AF.Exp), r=["bend"], w=["ast"])
            P.op("dve", lambda e: e.tensor_tensor(out=pmT[:], in0=misc[:, 128:256], in1=DT[:], op=ALU.mult), r=["DT"], x=["misc"], w=["pmT"])
            for j in range(2):
                P.op("dve", lambda e, j=j, cc=cc: e.tensor_tensor(out=qs[:, j, :], in0=qT[:, j, cc], in1=Arow[:], op=ALU.mult), r=["qT", "Arow"], w=["qs"])
            for j in range(2):
                P.op("pe", lambda e, j=j: e.matmul(outp[:], lhsT=qs[:, j, :], rhs=Cb[:, j, 0:512], start=(j == 0), stop=False), r=["qs", "Cb"], w=["outp"], sig=False)
            P.op("pe", lambda e, tt=tt: e.matmul(outp[:], lhsT=pmT[:], rhs=vaug[:, tt, 0:512], start=False, stop=True), r=["pmT", "vaug"], w=["outp"])
            for j in range(2):
                P.op("pe", lambda e, j=j: e.matmul(misc[:, 256:257], lhsT=qs[:, j, :], rhs=Cb[:, j, 512:513], start=(j == 0), stop=False), r=["qs", "Cb"], w=["misc"], sig=False)
            P.op("pe", lambda e, tt=tt: e.matmul(misc[:, 256:257], lhsT=pmT[:], rhs=vaug[:, tt, 512:513], start=False, stop=True), r=["pmT", "vaug"], w=["misc"])
            P.op("act", lambda e: e.activation(out=rr[:], in_=misc[:, 256:257], func=AF.Abs), x=["misc"], w=["rr"])
            P.op("dve", lambda e: e.tensor_scalar(out=rr[:], in0=rr[:], scalar1=1.0, scalar2=None, op0=ALU.max), r=["rr"], w=["rr"])
            P.op("dve", lambda e: e.reciprocal(out=rr[:], in_=rr[:]), r=["rr"], w=["rr"])
            P.op("act", lambda e: e.activation(out=hcs[:], in_=outp[:], func=AF.Copy, scale=rr[:, 0:1]), r=["rr"], x=["outp"], w=["hcs"])
            P.op("act", lambda e: e.activation(out=junk[:], in_=hcs[:], func=AF.Square, accum_out=ss[:]), r=["hcs"], w=["junk", "ss"])
            P.op("dve", lambda e: e.tensor_scalar(out=rstd[:], in0=ss[:], scalar1=1.0 / 512.0, scalar2=EPS, op0=ALU.mult, op1=ALU.add), r=["ss"], w=["rstd"])
            P.op("act", lambda e: e.activation(out=rstd[:], in_=rstd[:], func=AF.Sqrt), r=["rstd"], w=["rstd"])
            P.op("dve", lambda e: e.reciprocal(out=rstd[:], in_=rstd[:]), r=["rstd"], w=["rstd"])
            P.op("dve", lambda e, tt=tt: e.scalar_tensor_tensor(out=ysb[:], in0=hcs[:], scalar=rstd[:, 0:1], in1=gsig[:, tt, :], op0=ALU.mult, op1=ALU.mult),
                 r=["hcs", "rstd", "gsig"], w=["ysb"])
            P.dma("sp", y[c0 + tt * 128:c0 + (tt + 1) * 128, :], ysb[:], r=["ysb"], w=["y"])
            P.op("dve", lambda e, tt=tt: e.tensor_scalar(out=vw[:], in0=vaug[:, tt, :], scalar1=wexp[:, 0:1], scalar2=None, op0=ALU.mult), r=["vaug", "wexp"], w=["vw"])
            for j in range(2):
                P.op("pe", lambda e, j=j, tt=tt: e.matmul(dC[j][:], lhsT=ktok[:, tt, j * 128:(j + 1) * 128], rhs=vw[:, 0:512], start=True, stop=True),
                     r=["ktok", "vw"], w=["dC%d" % j])
                P.op("pe", lambda e, j=j, tt=tt: e.matmul(misc[:, 258 + j:259 + j], lhsT=ktok[:, tt, j * 128:(j + 1) * 128], rhs=vw[:, 512:513], start=True, stop=True),
                     r=["ktok", "vw"], w=["misc"])
            for j in range(2):
                P.op("dve", lambda e, j=j: e.scalar_tensor_tensor(out=Cf[:, j, 0:512], in0=Cf[:, j, 0:512], scalar=ast[:, 0:1], in1=dC[j][:], op0=ALU.mult, op1=ALU.add),
                     r=["Cf", "ast"], x=["dC%d" % j], w=["Cf"])
                P.op("dve", lambda e, j=j: e.scalar_tensor_tensor(out=Cf[:, j, 512:513], in0=Cf[:, j, 512:513], scalar=ast[:, 0:1], in1=misc[:, 258 + j:259 + j], op0=ALU.mult, op1=ALU.add),
                     r=["Cf", "ast"], x=["misc"], w=["Cf"])
                P.op("act", lambda e, j=j: e.activation(out=Cb[:, j, :], in_=Cf[:, j, :], func=AF.Copy), r=["Cf"], w=["Cb"])
    P.wait_all("sp", ["y"])
    P.emit()
    P.close()
    return nc


def build_gd(S):
    nc = bass.Bass("TRN2", target_bir_lowering=False)
    dr = lambda n, s, d, k="ExternalInput": nc.dram_tensor(n, list(s), d, kind=k).ap()
    hT = dr("hT", [2048, S], BF16)
    w_fm = dr("w_fm", [2048, 2048], F32)
    w_tm = dr("w_tm", [2048, 1040], F32)
    cwd = dr("cw", [128, 16, 4], F32)
    alogd = dr("alog", [128, 8], F32)
    dtbd = dr("dtb", [128, 8], F32)
    ngd = dr("ng", [128, 128], F32)
    trid = dr("tri", [128, 128], F32)
    nmd = dr("nmle", [128, 128], F32)
    mltd = dr("mlt", [128, 128], F32)
    moffd = dr("moff", [128, 128], F32)
    idfd = dr("idf", [128, 128], F32)
    idbd = dr("idb", [128, 128], BF16)
    y = dr("y", [S, 1024], BF16, "ExternalOutput")
    P = Prog(nc)
    wfm = P.sb("wfm", [128, 16, 2048], BF16)
    wtm = P.sb("wtm", [128, 16, 1040], BF16)
    for k0 in range(0, 16, 2):
        P.dma("pool", wfm[:, k0:k0 + 2, :], w_fm[k0 * 128:(k0 + 2) * 128, :].rearrange("(k p) m -> p k m", p=128), w=["wfm"])
        P.dma("pool", wtm[:, k0:k0 + 2, :], w_tm[k0 * 128:(k0 + 2) * 128, :].rearrange("(k p) m -> p k m", p=128), w=["wtm"])
    cst = {}
    for n, d, shp, dt_ in (("cw", cwd, [128, 16, 4], F32), ("alog", alogd, [128, 8], F32), ("dtb", dtbd, [128, 8], F32), ("ng", ngd, [128, 128], F32),
                           ("tri", trid, [128, 128], F32), ("nmle", nmd, [128, 128], F32), ("mlt", mltd, [128, 128], F32), ("moff", moffd, [128, 128], F32),
                           ("idf", idfd, [128, 128], F32), ("idb", idbd, [128, 128], BF16)):
        cst[n] = P.sb(n + "_sb", shp, dt_)
        P.dma("sp", cst[n][:], d, w=[n])
    cw, alog, dtb, ng, tri, nmle, mlt, idf, idb = [cst[n] for n in ("cw", "alog", "dtb", "ng", "tri", "nmle", "mlt", "idf", "idb")]
    moff = cst["moff"]
    onesf = P.sb("onesf", [128, 128], F32)
    onesb = P.sb("onesb", [128, 128], BF16)
    P.op("dve", lambda e: e.memset(onesf[:], 1.0), w=["onesf"])
    P.op("dve", lambda e: e.memset(onesb[:], 1.0), w=["onesb"])
    nea = P.sb("nea", [128, 8], F32)
    P.op("act", lambda e: e.activation(out=nea[:], in_=alog[:], func=AF.Exp), r=["alog"], w=["nea"])
    P.op("dve", lambda e: e.tensor_scalar(out=nea[:], in0=nea[:], scalar1=-1.0, scalar2=None, op0=ALU.mult), r=["nea"], w=["nea"])
    hb = P.sb("hb", [128, 16, 512], BF16)
    pre1 = P.sb("pre", [128, 515], F32)
    hist = P.sb("hist", [128, 16, 3], F32)
    P.op("dve", lambda e: e.memset(hist[:], 0.0), w=["hist"])
    acc = P.sb("acc", [128, 512], F32)
    cvo = P.sb("cvo", [128, 512], F32)
    sqb = P.sb("sqb", [128, 512], BF16)
    rn = P.sb("rn", [128, 512], F32)
    qT = P.sb("qT", [128, 4, 512], BF16)
    kT = P.sb("kT", [128, 4, 512], BF16)
    vT = P.sb("vT", [128, 8, 512], BF16)
    vtok = P.sb("vtok", [128, 4, 1024], BF16)
    ktok = P.sb("ktok", [128, 4, 512], BF16)
    zs = P.sb("zs", [128, 4, 1024], BF16)
    beta = P.sb("beta", [128, 4, 8], F32)
    gg = P.sb("gg", [128, 4, 8], F32)
    gcol = P.sb("gcol", [128, 4, 8], F32)
    Sf = P.sb("Sf", [128, 8, 128], F32)
    Sb = P.sb("Sb", [128, 8, 128], BF16)
    P.op("dve", lambda e: e.memset(Sf[:], 0.0), w=["Sf"])
    P.op("dve", lambda e: e.memset(Sb[:], 0.0), w=["Sb"])
    pf = P.ps("pf", [128, 512], F32)
    pt = P.ps("pt", [128, 512], F32)
    misc = P.ps("misc", [128, 512], F32)
    kk = P.ps("kk", [128, 512], F32)
    sol = P.ps("sol", [128, 512], F32)
    sqp = P.ps("sqp", [128, 512], F32)
    rec = P.ps("rec", [128, 512], F32)
    tpp = P.ps("tpp", [128, 1024], BF16)
    sm = lambda n: P.sb(n, [128, 1], F32)
    egc, glast, el, ekd, ss, rstd = [sm(n) for n in ("egc", "glast", "el", "ekd", "ss", "rstd")]
    LF = P.sb("LF", [128, 128], F32)
    arg = P.sb("arg", [128, 128], F32)
    E1 = P.sb("E1", [128, 128], F32)
    E2 = P.sb("E2", [128, 128], F32)
    Erow = P.sb("Erow", [128, 128], F32)
    AqkT = P.sb("AqkT", [128, 128], BF16)
    PT = [P.sb("PT%d" % i, [128, 128], F32) for i in range(6)]
    AoT = P.sb("AoT", [128, 128], F32)
    Zb = P.sb("Zb", [128, 256], F32)
    Pm = [P.sb("Pm%d" % i, [128, 128], F32) for i in range(2)]
    X = P.sb("X", [128, 256], F32)
    Ub = P.sb("Ub", [128, 128], F32)
    Wb = P.sb("Wb", [128, 128], F32)
    WT = P.sb("WT", [128, 128], F32)
    vn = P.sb("vn", [128, 128], BF16)
    qg = P.sb("qg", [128, 128], BF16)
    kd = P.sb("kd", [128, 128], BF16)
    of = P.sb("of", [128, 128], F32)
    junk = P.sb("junk", [128, 128], F32)
    ysb = P.sb("ysb", [128, 1024], BF16)
    tmpz = P.sb("tmpz", [128, 16], F32)

    for blk in range(S // 512):
        c0 = blk * 512
        P.dma("sp", hb[:], hT[:, c0:c0 + 512].rearrange("(t p) n -> p t n", p=128), w=["hb"])
        for mt in range(16):
            for kt in range(16):
                P.op("pe", lambda e, mt=mt, kt=kt: e.matmul(pf[:], lhsT=wfm[:, kt, mt * 128:(mt + 1) * 128], rhs=hb[:, kt, :],
                     start=(kt == 0), stop=(kt == 15)), r=["wfm", "hb"], w=["pf"], sig=(kt == 15))
            P.op("dve", lambda e, mt=mt: e.tensor_copy(out=pre1[:, 0:3], in_=hist[:, mt, :]), r=["hist"], w=["pre"])
            P.op("act", lambda e, mt=mt: e.activation(out=pre1[:, 3:515], in_=pf[:], func=AF.Copy), x=["pf"], w=["pre"])
            P.op("dve", lambda e, mt=mt: e.tensor_scalar(out=acc[:], in0=pre1[:, 0:512], scalar1=cw[:, mt, 0:1], scalar2=None, op0=ALU.mult),
                 r=["pre", "cw"], w=["acc"])
            for j in range(1, 4):
                P.op("dve", lambda e, mt=mt, j=j: e.scalar_tensor_tensor(out=acc[:], in0=pre1[:, j:j + 512], scalar=cw[:, mt, j:j + 1], in1=acc[:],
                     op0=ALU.mult, op1=ALU.add), r=["pre", "cw", "acc"], w=["acc"])
            P.op("dve", lambda e, mt=mt: e.tensor_copy(out=hist[:, mt, :], in_=pre1[:, 512:515]), r=["pre"], w=["hist"])
            if mt >= 8:
                P.op("act", lambda e, mt=mt: e.activation(out=vT[:, mt - 8, :], in_=acc[:], func=AF.Silu), r=["acc"], w=["vT"])
            else:
                P.op("act", lambda e: e.activation(out=cvo[:], in_=acc[:], func=AF.Silu), r=["acc"], w=["cvo"])
                P.op("act", lambda e: e.activation(out=sqb[:], in_=cvo[:], func=AF.Square), r=["cvo"], w=["sqb"])
                P.op("pe", lambda e: e.matmul(pt[:], lhsT=onesb[:], rhs=sqb[:], start=True, stop=True), r=["onesb", "sqb"], w=["pt"])
                P.op("dve", lambda e: e.tensor_scalar(out=rn[:], in0=pt[:], scalar1=EPS, scalar2=None, op0=ALU.add), x=["pt"], w=["rn"])
                P.op("act", lambda e: e.activation(out=rn[:], in_=rn[:], func=AF.Sqrt), r=["rn"], w=["rn"])
                P.op("dve", lambda e: e.reciprocal(out=rn[:], in_=rn[:]), r=["rn"], w=["rn"])
                if mt < 4:
                    P.op("dve", lambda e, mt=mt: e.scalar_tensor_tensor(out=qT[:, mt, :], in0=cvo[:], scalar=128 ** -0.5, in1=rn[:], op0=ALU.mult, op1=ALU.mult),
                         r=["cvo", "rn"], w=["qT"])
                else:
                    P.op("dve", lambda e, mt=mt: e.tensor_tensor(out=kT[:, mt - 4, :], in0=cvo[:], in1=rn[:], op=ALU.mult), r=["cvo", "rn"], w=["kT"])
        for tt in range(4):
            cc = slice(tt * 128, (tt + 1) * 128)
            for i in range(4):
                P.op("pe", lambda e, i=i, cc=cc: e.transpose(out=tpp[:, i * 128:(i + 1) * 128], in_=kT[:, i, cc], identity=idb[:]), r=["kT", "idb"], w=["tpp"])
            P.op("act", lambda e, tt=tt: e.activation(out=ktok[:, tt, :], in_=tpp[:, 0:512], func=AF.Copy), x=["tpp"], w=["ktok"])
            for i in range(8):
                P.op("pe", lambda e, i=i, cc=cc: e.transpose(out=tpp[:, i * 128:(i + 1) * 128], in_=vT[:, i, cc], identity=idb[:]), r=["vT", "idb"], w=["tpp"])
            P.op("act", lambda e, tt=tt: e.activation(out=vtok[:, tt, :], in_=tpp[:, 0:1024], func=AF.Copy), x=["tpp"], w=["vtok"])
        for tt in range(4):
            for (g0, gn) in ((0, 512), (512, 512), (1024, 16)):
                for kt in range(16):
                    P.op("pe", lambda e, tt=tt, kt=kt, g0=g0, gn=gn: e.matmul(pt[:, 0:gn], lhsT=hb[:, kt, tt * 128:(tt + 1) * 128],
                         rhs=wtm[:, kt, g0:g0 + gn], start=(kt == 0), stop=(kt == 15)), r=["wtm", "hb"], w=["pt"], sig=(kt == 15))
                if g0 < 1024:
                    P.op("act", lambda e, tt=tt, g0=g0: e.activation(out=zs[:, tt, g0:g0 + 512], in_=pt[:], func=AF.Silu), x=["pt"], w=["zs"])
                else:
                    P.op("act", lambda e, tt=tt: e.activation(out=beta[:, tt, :], in_=pt[:, 0:8], func=AF.Sigmoid), x=["pt"], w=["beta"])
                    P.op("dve", lambda e: e.tensor_tensor(out=tmpz[:, 0:8], in0=pt[:, 8:16], in1=dtb[:], op=ALU.add), r=["dtb"], x=["pt"], w=["tmpz"])
                    P.op("act", lambda e: e.activation(out=tmpz[:, 0:8], in_=tmpz[:, 0:8], func=AF.Exp), r=["tmpz"], w=["tmpz"])
                    P.op("act", lambda e: e.activation(out=tmpz[:, 8:16], in_=tmpz[:, 0:8], func=AF.Ln, bias=1.0), r=["tmpz"], w=["tmpz"])
                    P.op("dve", lambda e, tt=tt: e.tensor_tensor(out=gg[:, tt, :], in0=tmpz[:, 8:16], in1=nea[:], op=ALU.mult), r=["tmpz", "nea"], w=["gg"])
            P.op("pe", lambda e, tt=tt: e.matmul(misc[:, 256:264], lhsT=tri[:], rhs=gg[:, tt, :], start=True, stop=True), r=["tri", "gg"], w=["misc"])
            P.op("dve", lambda e, tt=tt: e.tensor_copy(out=gcol[:, tt, :], in_=misc[:, 256:264]), x=["misc"], w=["gcol"])
        for tt in range(4):
            cc = slice(tt * 128, (tt + 1) * 128)
            for h in range(8):
                kh = h // 2
                gc = gcol[:, tt, h:h + 1]
                bcol = beta[:, tt, h:h + 1]
                P.op("dve", lambda e, tt=tt, h=h: e.tensor_scalar(out=LF[:], in0=onesf[:], scalar1=gg[:, tt, h:h + 1], scalar2=None, op0=ALU.mult), r=["gg", "onesf"], w=["LF"])
                P.op("pe", lambda e: e.matmul(misc[:, 0:128], lhsT=LF[:], rhs=tri[:], start=True, stop=True), r=["LF", "tri"], w=["misc"])
                P.op("dve", lambda e, gc=gc: e.scalar_tensor_tensor(out=arg[:], in0=misc[:, 0:128], scalar=gc, in1=nmle[:], op0=ALU.subtract, op1=ALU.add),
                     r=["gcol", "nmle"], x=["misc"], w=["arg"])
                P.op("dve", lambda e: e.tensor_copy(out=glast[:], in_=misc[:, 127:128]), x=["misc"], w=["glast"])
                P.op("act", lambda e: e.activation(out=Erow[:], in_=misc[:, 0:128], func=AF.Exp), x=["misc"], w=["Erow"])
                P.op("act", lambda e: e.activation(out=E1[:], in_=arg[:], func=AF.Exp), r=["arg"], w=["E1"])
                P.op("act", lambda e, gc=gc: e.activation(out=egc[:], in_=gc, func=AF.Exp), r=["gcol"], w=["egc"])
                P.op("act", lambda e: e.activation(out=el[:], in_=glast[:], func=AF.Exp), r=["glast"], w=["el"])
                P.op("act", lambda e, gc=gc: e.activation(out=ekd[:], in_=gc, func=AF.Exp, scale=-1.0, bias=glast[:, 0:1]), r=["gcol", "glast"], w=["ekd"])
                P.op("dve", lambda e: e.tensor_tensor(out=E2[:], in0=E1[:], in1=mlt[:], op=ALU.mult), r=["E1", "mlt"], w=["E2"])
                P.op("pe", lambda e, kh=kh, cc=cc: e.matmul(kk[:, 0:128], lhsT=kT[:, kh, cc], rhs=kT[:, kh, cc], start=True, stop=True), r=["kT"], w=["kk"])
                P.op("pe", lambda e, kh=kh, cc=cc: e.matmul(kk[:, 128:256], lhsT=kT[:, kh, cc], rhs=qT[:, kh, cc], start=True, stop=True), r=["kT", "qT"], w=["kk"])
                P.op("dve", lambda e: e.tensor_tensor(out=AqkT[:], in0=kk[:, 128:256], in1=E1[:], op=ALU.mult), r=["E1"], x=["kk"], w=["AqkT"])
                P.op("dve", lambda e: e.tensor_tensor(out=PT[0][:], in0=kk[:, 0:128], in1=E2[:], op=ALU.mult), r=["E2"], x=["kk"], w=["PT0"])
                P.op("dve", lambda e, bcol=bcol: e.tensor_scalar(out=PT[0][:], in0=PT[0][:], scalar1=bcol, scalar2=None, op0=ALU.mult), r=["PT0", "beta"], w=["PT0"])
                P.op("dve", lambda e: e.tensor_tensor(out=AoT[:], in0=E1[:], in1=moff[:], op=ALU.mult), r=["E1", "moff"], w=["AoT"])
                P.op("dve", lambda e: e.tensor_tensor(out=AoT[:], in0=kk[:, 0:128], in1=AoT[:], op=ALU.mult), r=["AoT"], x=["kk"], w=["AoT"])
                P.op("dve", lambda e, bcol=bcol: e.tensor_scalar(out=AoT[:], in0=AoT[:], scalar1=bcol, scalar2=None, op0=ALU.mult), r=["AoT", "beta"], w=["AoT"])
                P.op("pe", lambda e: e.matmul(sqp[:, 0:128], lhsT=PT[0][:], rhs=idf[:], start=True, stop=True), r=["PT0", "idf"], w=["sqp"])
                P.op("act", lambda e: e.activation(out=Pm[0][:], in_=sqp[:, 0:128], func=AF.Copy), x=["sqp"], w=["Pm0"])
                P.op("dve", lambda e, tt=tt, h=h: e.tensor_copy(out=X[:, 0:128], in_=vtok[:, tt, h * 128:(h + 1) * 128]), r=["vtok"], w=["X"])
                P.op("dve", lambda e, tt=tt, kh=kh: e.tensor_scalar(out=X[:, 128:256], in0=ktok[:, tt, kh * 128:(kh + 1) * 128], scalar1=egc[:, 0:1], scalar2=None, op0=ALU.mult),
                     r=["ktok", "egc"], w=["X"])
                for j in range(6):
                    a, b = j % 2, (j + 1) % 2
                    P.op("pe", lambda e, j=j: e.matmul(sol[:, 0:256], lhsT=PT[j][:], rhs=X[:], start=True, stop=True), r=["PT%d" % j, "X"], w=["sol"])
                    P.op("dve", lambda e, j=j: e.tensor_tensor(out=X[:], in0=X[:], in1=sol[:, 0:256], op=(ALU.subtract if j == 0 else ALU.add)), r=["X"], x=["sol"], w=["X"])
                    if j < 5:
                        P.op("pe", lambda e, j=j, a=a: e.matmul(sqp[:, 0:128], lhsT=PT[j][:], rhs=Pm[a][:], start=True, stop=True), r=["PT%d" % j, "Pm%d" % a], w=["sqp"])
                        P.op("pe", lambda e, j=j, a=a: e.matmul(sqp[:, 128:256], lhsT=Pm[a][:], rhs=PT[j][:], start=True, stop=True), r=["PT%d" % j, "Pm%d" % a], w=["sqp"])
                        P.op("act", lambda e, b=b: e.activation(out=Pm[b][:], in_=sqp[:, 0:128], func=AF.Copy), x=["sqp"], w=["Pm%d" % b])
                        P.op("act", lambda e, j=j: e.activation(out=PT[j + 1][:], in_=sqp[:, 128:256], func=AF.Copy), x=["sqp"], w=["PT%d" % (j + 1)])
                P.op("pe", lambda e: e.matmul(sol[:, 0:256], lhsT=AoT[:], rhs=X[:], start=True, stop=True), r=["AoT", "X"], w=["sol"])
                P.op("act", lambda e: e.activation(out=Zb[:], in_=sol[:, 0:256], func=AF.Copy), x=["sol"], w=["Zb"])
                for j in range(6):
                    P.op("pe", lambda e, j=j: e.matmul(sol[:, 0:256], lhsT=PT[j][:], rhs=Zb[:], start=True, stop=True), r=["PT%d" % j, "Zb"], w=["sol"])
                    P.op("dve", lambda e, j=j: e.tensor_tensor(out=Zb[:], in0=Zb[:], in1=sol[:, 0:256], op=(ALU.subtract if j == 0 else ALU.add)), r=["Zb"], x=["sol"], w=["Zb"])
                P.op("dve", lambda e: e.tensor_tensor(out=X[:], in0=X[:], in1=Zb[:], op=ALU.subtract), r=["X", "Zb"], w=["X"])
                P.op("dve", lambda e, bcol=bcol: e.tensor_scalar(out=Ub[:], in0=X[:, 0:128], scalar1=bcol, scalar2=None, op0=ALU.mult), r=["X", "beta"], w=["Ub"])
                P.op("dve", lambda e, bcol=bcol: e.tensor_scalar(out=Wb[:], in0=X[:, 128:256], scalar1=bcol, scalar2=None, op0=ALU.mult), r=["X", "beta"], w=["Wb"])
                P.op("pe", lambda e: e.matmul(sol[:, 256:384], lhsT=Wb[:], rhs=idf[:], start=True, stop=True), r=["Wb", "idf"], w=["sol"])
                P.op("act", lambda e: e.activation(out=WT[:], in_=sol[:, 256:384], func=AF.Copy), x=["sol"], w=["WT"])
                P.op("pe", lambda e, h=h: e.matmul(rec[:, 0:128], lhsT=WT[:], rhs=Sf[:, h, :], start=True, stop=True), r=["WT", "Sf"], w=["rec"])
                P.op("dve", lambda e: e.tensor_tensor(out=vn[:], in0=Ub[:], in1=rec[:, 0:128], op=ALU.subtract), r=["Ub"], x=["rec"], w=["vn"])
                P.op("dve", lambda e, kh=kh, cc=cc: e.tensor_tensor(out=qg[:], in0=qT[:, kh, cc], in1=Erow[:], op=ALU.mult), r=["qT", "Erow"], w=["qg"])
                P.op("pe", lambda e, h=h: e.matmul(rec[:, 128:256], lhsT=qg[:], rhs=Sb[:, h, :], start=True, stop=False), r=["qg", "Sb"], w=["rec"], sig=False)
                P.op("pe", lambda e: e.matmul(rec[:, 128:256], lhsT=AqkT[:], rhs=vn[:], start=False, stop=True), r=["AqkT", "vn"], w=["rec"])
                P.op("dve", lambda e, tt=tt, kh=kh: e.tensor_scalar(out=kd[:], in0=ktok[:, tt, kh * 128:(kh + 1) * 128], scalar1=ekd[:, 0:1], scalar2=None, op0=ALU.mult),
                     r=["ktok", "ekd"], w=["kd"])
                P.op("pe", lambda e: e.matmul(rec[:, 256:384], lhsT=kd[:], rhs=vn[:], start=True, stop=True), r=["kd", "vn"], w=["rec"])
                P.op("dve", lambda e, h=h: e.scalar_tensor_tensor(out=Sf[:, h, :], in0=Sf[:, h, :], scalar=el[:, 0:1], in1=rec[:, 256:384], op0=ALU.mult, op1=ALU.add),
                     r=["Sf", "el"], x=["rec"], w=["Sf"])
                P.op("act", lambda e, h=h: e.activation(out=Sb[:, h, :], in_=Sf[:, h, :], func=AF.Copy), r=["Sf"], w=["Sb"])
                P.op("act", lambda e: e.activation(out=of[:], in_=rec[:, 128:256], func=AF.Copy), x=["rec"], w=["of"])
                P.op("act", lambda e: e.activation(out=junk[:], in_=of[:], func=AF.Square, accum_out=ss[:]), r=["of"], w=["junk", "ss"])
                P.op("dve", lambda e: e.tensor_scalar(out=rstd[:], in0=ss[:], scalar1=1.0 / 128.0, scalar2=EPS, op0=ALU.mult, op1=ALU.add), r=["ss"], w=["rstd"])
                P.op("act", lambda e: e.activation(out=rstd[:], in_=rstd[:], func=AF.Sqrt), r=["rstd"], w=["rstd"])
                P.op("dve", lambda e: e.reciprocal(out=rstd[:], in_=rstd[:]), r=["rstd"], w=["rstd"])
                P.op("dve", lambda e: e.scalar_tensor_tensor(out=of[:], in0=of[:], scalar=rstd[:, 0:1], in1=ng[:], op0=ALU.mult, op1=ALU.mult), r=["of", "rstd", "ng"], w=["of"])
                P.op("dve", lambda e, tt=tt, h=h: e.tensor_tensor(out=ysb[:, h * 128:(h + 1) * 128], in0=of[:], in1=zs[:, tt, h * 128:(h + 1) * 128], op=ALU.mult),
                     r=["of", "zs"], w=["ysb"])
            P.dma("sp", y[c0 + tt * 128:c0 + (tt + 1) * 128, :], ysb[:], r=["ysb"], w=["y"])
    P.wait_all("sp", ["y"])
    P.emit()
    P.close()
    return nc


_DBG = {}


def _gT(rows):
    rows = list(rows) + [np.zeros(2048, np.float32)] * (8 - len(rows))
    return np.ascontiguousarray(np.stack(rows).astype(np.float32).reshape(8, 16, 128).transpose(2, 0, 1))


def kernel(x, mem, norm_g, mem_norm_g, w_mem_kv, w_xq, w_xo, w_up, w_down,
           mlstm_w_in, mlstm_b_gates, mlstm_head_g, mlstm_w_out,
           gdn_w_in, gdn_conv_w, gdn_a_log, gdn_dt_bias, gdn_norm_g, gdn_w_out):
    bf = ml_dtypes.bfloat16
    f = lambda a: np.ascontiguousarray(np.asarray(a, dtype=np.float32))
    x, mem, norm_g, mem_norm_g = f(x), f(mem), f(norm_g), f(mem_norm_g)
    B, S, D = x.shape
    NT = S // 4
    cores = list(range(8))
    eye = np.eye(128, dtype=np.float32)
    tri = np.triu(np.ones((128, 128), np.float32))
    nmle = np.where(tri > 0, 0.0, -30000.0).astype(np.float32)
    _ii = np.arange(128)
    mlt = ((_ii[:, None] < _ii[None, :]) & ((_ii[:, None] // 64) == (_ii[None, :] // 64))).astype(np.float32)
    moff = ((_ii[:, None] < 64) & (_ii[None, :] >= 64)).astype(np.float32)
    bc = lambda v: np.ascontiguousarray(np.broadcast_to(v, (128,) + v.shape))
    xT = [np.ascontiguousarray(x[c // 4, (c % 4) * NT:(c % 4 + 1) * NT].T) for c in cores]
    g0 = _gT([norm_g[0, 0]])
    r = run_bass_kernel_spmd(build_a0(NT), [{"xT": xT[c], "gT": g0} for c in cores], core_ids=cores)
    hT = [np.ascontiguousarray(np.concatenate([r.results[b * 4 + s]["ho"] for s in range(4)], axis=1)) for b in range(B)]
    _DBG["h0"] = hT
    W = f(mlstm_w_in)[0]; bgs = f(mlstm_b_gates)[0]; hgs = f(mlstm_head_g)[0]
    ins = []
    for c in cores:
        b, hd = c // 4, c % 4
        wq = W[:, hd * 256:(hd + 1) * 256]; wk = W[:, 1024 + hd * 256:1024 + (hd + 1) * 256]
        wv = W[:, 2048 + hd * 512:2048 + (hd + 1) * 512]; wo = W[:, 4096 + hd * 512:4096 + (hd + 1) * 512]
        wg = W[:, [6144 + hd, 6148 + hd]]
        ins.append(dict(hT=hT[b], w_fm=np.ascontiguousarray(np.concatenate([wq, wk], 1)),
                        w_tm=np.ascontiguousarray(np.concatenate([wv, wo, wk, wg], 1)),
                        bg=bc(np.array([bgs[hd], bgs[4 + hd]], np.float32)), hg=bc(hgs[hd * 512:(hd + 1) * 512]), tri=tri, negmask=nmle))
    r = run_bass_kernel_spmd(build_ml(S), ins, core_ids=cores)
    ycat = [np.concatenate([r.results[b * 4 + hd]["y"] for hd in range(4)], axis=1) for b in range(B)]
    _DBG["y0"] = ycat
    memT = [np.ascontiguousarray(mem[b].T) for b in range(B)]
    idb = eye.astype(bf)

    def tl_ins(li, yc, xTs, w_out):
        g = _gT([norm_g[li, 1], norm_g[li, 2], norm_g[li, 3], norm_g[li, 4], norm_g[li, 5], norm_g[(li + 1) % 2, 0], mem_norm_g])
        out = []
        for c in cores:
            b, s = c // 4, c % 4
            out.append(dict(yT=np.ascontiguousarray(yc[b][s * NT:(s + 1) * NT].T), xT=xTs[c], memT=memT[b], gT=g, w_out=w_out,
                            w_xq=f(w_xq)[li], w_xo=f(w_xo)[li], w_up=f(w_up)[li], w_down=f(w_down)[li], w_kv=f(w_mem_kv), identd=idb))
        return out
    r = run_bass_kernel_spmd(build_tl(NT, 2048, last=False), tl_ins(0, ycat, xT, f(mlstm_w_out)[0]), core_ids=cores)
    xT = [r.results[c]["xo"] for c in cores]
    hT = [np.ascontiguousarray(np.concatenate([r.results[b * 4 + s]["ho"] for s in range(4)], axis=1)) for b in range(B)]
    _DBG["x0"] = xT; _DBG["h1"] = hT
    W = f(gdn_w_in)[0]; CW = f(gdn_conv_w)[0]
    ins = []
    for c in cores:
        b, hg = c // 4, c % 4
        cq = np.arange(hg * 512, (hg + 1) * 512); ck = 2048 + cq; cv = 4096 + np.arange(hg * 1024, (hg + 1) * 1024)
        cc = np.concatenate([cq, ck, cv])
        cz = 8192 + np.arange(hg * 1024, (hg + 1) * 1024); cb = 12288 + np.arange(hg * 8, (hg + 1) * 8); ca = 12320 + np.arange(hg * 8, (hg + 1) * 8)
        ins.append(dict(hT=hT[b], w_fm=np.ascontiguousarray(W[:, cc]), w_tm=np.ascontiguousarray(W[:, np.concatenate([cz, cb, ca])]),
                        cw=np.ascontiguousarray(CW[:, cc].reshape(4, 16, 128).transpose(2, 1, 0)),
                        alog=bc(f(gdn_a_log)[0][hg * 8:(hg + 1) * 8]), dtb=bc(f(gdn_dt_bias)[0][hg * 8:(hg + 1) * 8]), ng=bc(f(gdn_norm_g)[0]),
                        tri=tri, nmle=nmle, mlt=mlt, moff=moff, idf=eye, idb=idb))
    r = run_bass_kernel_spmd(build_gd(S), ins, core_ids=cores)
    ycat = [np.concatenate([r.results[b * 4 + hg]["y"] for hg in range(4)], axis=1) for b in range(B)]
    _DBG["y1"] = ycat
    r = run_bass_kernel_spmd(build_tl(NT, 4096, last=True), tl_ins(1, ycat, xT, f(gdn_w_out)[0]), core_ids=cores)
    out = np.empty((B, S, D), np.float32)
    for c in cores:
        out[c // 4, (c % 4) * NT:(c % 4 + 1) * NT] = r.results[c]["xo"].T
    return out
```

```python
import ml_dtypes
import contextlib
import numpy as np
import concourse.bass as bass
import concourse.mybir as mybir
from concourse.bass_utils import run_bass_kernel_spmd

F32 = mybir.dt.float32
BF16 = mybir.dt.bfloat16
AF = mybir.ActivationFunctionType
ALU = mybir.AluOpType
AX = mybir.AxisListType


class Prog:
    ENGS = ("pe", "act", "dve", "pool", "sp")

    def __init__(self, nc):
        self.nc = nc
        self.es = contextlib.ExitStack()
        self.pes = contextlib.ExitStack()
        self.ops = {e: [] for e in self.ENGS}
        self.cnt = {e: 0 for e in self.ENGS}
        self.sem = {}
        for e in ("pe", "act", "dve", "pool"):
            self.sem[e] = self.es.enter_context(nc.semaphore("s_" + e))
        self.dsem = {}
        self.seen = {e: {} for e in self.ENGS}
        self.lastw = {}
        self.readers = {}
        self.nbuf = 0

    def sb(self, name, shape, dt):
        self.nbuf += 1
        return self.pes.enter_context(self.nc.sbuf_tensor("%s_%d" % (name, self.nbuf), list(shape), dt))

    def ps(self, name, shape, dt):
        self.nbuf += 1
        return self.pes.enter_context(self.nc.psum_tensor("%s_%d" % (name, self.nbuf), list(shape), dt))

    def barrier(self):
        evs = [(self.sem[f], self.cnt[f]) for f in ("pe", "act", "dve", "pool") if self.cnt[f] > 0]
        evs += [(ds[0], ds[1]) for ds in self.dsem.values() if ds[1] > 0]
        for e in self.ENGS:
            seen = self.seen[e]
            waits = []
            for (s_, v) in evs:
                if e in self.sem and s_ is self.sem[e]:
                    continue
                if seen.get(id(s_), 0) >= v:
                    continue
                seen[id(s_)] = v
                waits.append((s_, v))
            self.ops[e].append((None, waits, None, 0))

    def end_phase(self):
        self.barrier()
        self.emit()
        self.ops = {e: [] for e in self.ENGS}
        self.pes.close()
        self.pes = contextlib.ExitStack()
        self.lastw = {k: v for k, v in self.lastw.items()}

    def _deps(self, eng, own_sem, r, w, x=()):
        deps = {}

        def add(ev, same_ok):
            if ev is None:
                return
            s, v = ev
            if s is own_sem and same_ok:
                return
            k = id(s)
            if k not in deps or deps[k][1] < v:
                deps[k] = (s, v)

        for k in r:
            add(self.lastw.get(k), False)
        for k in x:
            add(self.lastw.get(k), False)
            for ev in self.readers.get(k, ()):
                add(ev, True)
        for k in w:
            add(self.lastw.get(k), True)
            for ev in self.readers.get(k, ()):
                add(ev, True)
        out = []
        seen = self.seen[eng]
        for k, (s, v) in deps.items():
            if seen.get(k, 0) >= v:
                continue
            seen[k] = v
            out.append((s, v))
        return out

    def _commit(self, ev, r, w):
        for k in r:
            self.readers.setdefault(k, []).append(ev)
        for k in w:
            self.lastw[k] = ev
            self.readers[k] = []

    def op(self, eng, fn, r=(), w=(), sig=True, x=()):
        own = self.sem[eng]
        waits = self._deps(eng, own, r, w, x)
        r = list(r) + list(x)
        if eng == "pe":
            waits = [(s, v) for (s, v) in waits if s is not own]
        ev = (own, self.cnt[eng] + 1)
        if sig:
            self.cnt[eng] += 1
        self.ops[eng].append((fn, waits, own if sig else None, 1))
        self._commit(ev, r, w)
        if sig and self.cnt[eng] >= 30000:
            self.sem[eng] = self.es.enter_context(self.nc.semaphore("s_%s_%d" % (eng, len(self.ops[eng]))))
            self.cnt[eng] = 0

    def dma(self, q, out, in_, r=(), w=(), key=None, **kw):
        if key is None:
            key = w[0] if w else r[0]
        if key not in self.dsem:
            self.dsem[key] = [self.es.enter_context(self.nc.semaphore("d%d" % len(self.dsem))), 0]
        ds = self.dsem[key]
        waits = self._deps(q, None, r, w)
        ds[1] += 16
        ev = (ds[0], ds[1])
        self.ops[q].append((lambda e: e.dma_start(out=out, in_=in_, **kw), waits, ds[0], 16))
        self._commit(ev, r, w)
        return ev

    def coll(self, kind, src, dst, groups, r=(), w=(), key=None):
        if key is None:
            key = w[0]
        if key not in self.dsem:
            self.dsem[key] = [self.es.enter_context(self.nc.semaphore("d%d" % len(self.dsem))), 0]
        ds = self.dsem[key]
        waits = self._deps("pool", None, r, w)
        ds[1] += 1
        ev = (ds[0], ds[1])
        self.ops["pool"].append((lambda e: e.collective_compute(kind, ALU.bypass, replica_groups=groups, ins=[src], outs=[dst]), waits, ds[0], 1))
        self._commit(ev, r, w)

    def wait_all(self, eng, keys):
        waits = self._deps(eng, None, keys, ())
        self.ops[eng].append((None, waits, None, 0))

    def emit(self):
        nc = self.nc
        engmap = {"pe": "tensor", "act": "scalar", "dve": "vector", "pool": "gpsimd", "sp": "sync"}
        with nc.Block() as block:
            for e in self.ENGS:
                ops = self.ops[e]
                if not ops:
                    continue

                def body(eh, ops=ops):
                    for fn, waits, s, inc in ops:
                        for (ws, wv) in waits:
                            eh.wait_ge(ws, wv)
                        if fn is not None:
                            ins = fn(eh)
                            if s is not None:
                                ins.then_inc(s, inc)

                getattr(block, engmap[e])(body)

    def close(self):
        self.pes.close()
        self.es.close()

import ml_dtypes

EPS = 1e-6
TB = 512


class TL:
    def __init__(self, P, nc):
        self.P = P
        self.nc = nc
        P_ = P
        self.lin_ps = [P_.ps("lps%d" % i, [128, 512], F32) for i in range(6)]
        self.st_ps = P_.ps("stps", [128, 512], F32)
        self.at_ps = P_.ps("atps", [128, 512], F32)
        self.NW = 3
        self.wring = [P_.sb("wr%d" % i, [128, 8, 256], BF16) for i in range(self.NW)]
        self.wcnt = 0
        self.setc = 0
        self.ones = P_.sb("ones", [128, 128], BF16)
        P_.op("dve", lambda e: e.memset(self.ones[:], 1.0), w=["ones"])
        self.ident = P_.sb("ident", [128, 128], BF16)
        self.sq = [P_.sb("sq%d" % i, [128, 512], BF16) for i in range(2)]
        self.sqc = 0
        self.rstd = P_.sb("rstd", [128, 512], F32)
        self.tmp = [P_.sb("tmp%d" % i, [128, 512], F32) for i in range(2)]
        self.tmpc = 0

    def linear(self, W, K, M, in_sb, in_key, evac, ntok=TB):
        P = self.P
        KT = K // 128
        kgs = [(k0, min(8, KT - k0)) for k0 in range(0, KT, 8)]
        for mg in range(M // 256):
            st = self.setc % 3
            self.setc += 1
            pk = ["lps%d" % (st * 2), "lps%d" % (st * 2 + 1)]
            for (k0, nk) in kgs:
                b = self.wcnt % self.NW
                self.wcnt += 1
                wk = "wr%d" % b
                src = W[k0 * 128:(k0 + nk) * 128, mg * 256:(mg + 1) * 256].rearrange("(k p) m -> p k m", p=128)
                P.dma("pool", self.wring[b][:, 0:nk, :], src, w=[wk])
                for k in range(nk):
                    for j in range(2):
                        kk = k0 + k
                        P.op("pe", lambda e, b=b, k=k, j=j, kk=kk, st=st: e.matmul(
                            self.lin_ps[st * 2 + j][:, 0:ntok], lhsT=self.wring[b][:, k, j * 128:(j + 1) * 128],
                            rhs=in_sb[:, kk, 0:ntok], start=(kk == 0), stop=(kk == KT - 1)),
                            r=[wk, in_key], w=[pk[j]], sig=(kk == KT - 1 or k == nk - 1))
            for j in range(2):
                evac(mg * 2 + j, self.lin_ps[st * 2 + j], pk[j])

    def sq_acc(self, src_ap, src_keys, idx, n, ntok=TB, psum=False):
        P = self.P
        b = self.sqc % 2
        self.sqc += 1
        sq = self.sq[b]
        P.op("act", lambda e: e.activation(out=sq[:, 0:ntok], in_=src_ap, func=AF.Square), r=([] if psum else src_keys), x=(src_keys if psum else []), w=["sq%d" % b])
        P.op("pe", lambda e: e.matmul(self.st_ps[:, 0:ntok], lhsT=self.ones[:], rhs=sq[:, 0:ntok], start=(idx == 0), stop=(idx == n - 1)),
             r=["sq%d" % b, "ones"], w=["stps"], sig=True)

    def make_rstd(self, nfeat, ntok=TB):
        P = self.P
        P.op("dve", lambda e: e.tensor_scalar(out=self.rstd[:, 0:ntok], in0=self.st_ps[:, 0:ntok], scalar1=1.0 / nfeat, scalar2=EPS,
                                              op0=ALU.mult, op1=ALU.add), x=["stps"], w=["rstd"])
        P.op("act", lambda e: e.activation(out=self.rstd[:, 0:ntok], in_=self.rstd[:, 0:ntok], func=AF.Sqrt), r=["rstd"], w=["rstd"])
        P.op("dve", lambda e: e.reciprocal(out=self.rstd[:, 0:ntok], in_=self.rstd[:, 0:ntok]), r=["rstd"], w=["rstd"])

    def norm_to_bf16(self, x_sb, xkey, g_sb, gi, h_sb, hkey, ntok=TB, nt=16):
        P = self.P
        for dt in range(nt):
            self.sq_acc(x_sb[:, dt, 0:ntok], [xkey], dt, nt, ntok)
        self.make_rstd(nt * 128, ntok)
        for dt in range(nt):
            P.op("dve", lambda e, dt=dt: e.scalar_tensor_tensor(out=h_sb[:, dt, 0:ntok], in0=x_sb[:, dt, 0:ntok],
                 scalar=g_sb[:, gi, dt:dt + 1], in1=self.rstd[:, 0:ntok], op0=ALU.mult, op1=ALU.mult),
                 r=[xkey, "rstd", "g"], w=[hkey])

    def normres(self, d_sb, dkey, g_sb, gi, x_sb, xkey, ntok=TB):
        P = self.P
        self.make_rstd(2048, ntok)
        for dt in range(16):
            b = self.tmpc % 2
            self.tmpc += 1
            t = self.tmp[b]
            P.op("dve", lambda e, dt=dt, t=t: e.tensor_tensor(out=t[:, 0:ntok], in0=d_sb[:, dt, 0:ntok],
                 in1=self.rstd[:, 0:ntok], op=ALU.mult), r=[dkey, "rstd"], w=["tmp%d" % b])
            P.op("dve", lambda e, dt=dt, t=t: e.scalar_tensor_tensor(out=x_sb[:, dt, 0:ntok], in0=t[:, 0:ntok],
                 scalar=g_sb[:, gi, dt:dt + 1], in1=x_sb[:, dt, 0:ntok], op0=ALU.mult, op1=ALU.add),
                 r=["tmp%d" % b, xkey, "g"], w=[xkey])

    def evac_d(self, d_sb, dkey, ntok=TB):
        P = self.P

        def ev(mt, ps, pk):
            P.op("dve", lambda e: e.tensor_copy(out=d_sb[:, mt, 0:ntok], in_=ps[:, 0:ntok]), x=[pk], w=[dkey])
            import os
            if os.environ.get("NOSQ") != "1":
                self.sq_acc(ps[:, 0:ntok], [pk], mt, 16, ntok, psum=True)
        return ev


def phase_tl(P, nc, NT, KY, last, D, S=None):
    yall, seld, xT, memT, gT, w_out, w_xq, w_xo, w_up, w_down, w_kv, identd, xo = [D[k] for k in
        ("yall", "sel", "xT", "memT", "gT", "w_out", "w_xq", "w_xo", "w_up", "w_down", "w_kv", "identd", "xo")]
    ho = None if last else D["ho"]
    stage = 99
    T = TL(P, nc)
    KYT = KY // 128
    g_sb = P.sb("g", [128, 8, 16], F32)
    P.dma("sp", g_sb[:], gT, w=["g"])
    P.dma("sp", T.ident[:], identd, w=["ident"])
    sel = P.sb("sel", [128, 4], F32)
    P.dma("sp", sel[:], seld, w=["sel"])
    x_sb = P.sb("x", [128, 16, TB], F32)
    d_sb = P.sb("d", [128, 16, TB], F32)
    h_sb = P.sb("h", [128, 16, TB], BF16)
    u_sb = P.sb("u", [128, 64, TB], BF16)
    q_sb = P.sb("q", [128, 4, TB], BF16)
    o_sb = P.sb("o", [128, 4, TB], BF16)
    kmT = P.sb("kmT", [128, 4, 256], BF16)
    vm = P.sb("vm", [128, 2, 512], BF16)
    P.dma("sp", x_sb[:, :, 0:256], memT.rearrange("(t p) m -> p t m", p=128), w=["x"])
    if stage >= 0.2:
        T.norm_to_bf16(x_sb, "x", g_sb, 6, h_sb, "h", ntok=256)

    def ev_k(mt, ps, pk):
        P.op("act", lambda e: e.activation(out=kmT[:, mt, :], in_=ps[:, 0:256], func=AF.Copy), x=[pk], w=["kmT"])
    if stage >= 0.3:
        T.linear(w_kv[:, 0:512], 2048, 512, h_sb, "h", ev_k, ntok=256)
    vT = P.sb("vT", [128, 4, 256], BF16)

    def ev_v(mt, ps, pk):
        P.op("act", lambda e: e.activation(out=vT[:, mt, :], in_=ps[:, 0:256], func=AF.Copy), x=[pk], w=["vT"])
    if stage >= 0.4:
        T.linear(w_kv[:, 512:1024], 2048, 512, h_sb, "h", ev_v, ntok=256)
    tp_ps = T.at_ps[:, 384:512].bitcast(BF16)
    for hd in range(4 if stage >= 1 else 0):
        for mt in range(2):
            P.op("pe", lambda e, hd=hd, mt=mt: e.transpose(out=tp_ps[:, 0:128], in_=vT[:, hd, mt * 128:(mt + 1) * 128], identity=T.ident[:]),
                 r=["vT", "ident"], w=["atps"])
            P.op("act", lambda e, hd=hd, mt=mt: e.activation(out=vm[:, mt, hd * 128:(hd + 1) * 128], in_=tp_ps[:, 0:128], func=AF.Copy),
                 x=["atps"], w=["vm"])
    SC = 128 ** -0.5
    mx = P.sb("mx", [128, 1], F32)
    rs = P.sb("rs", [128, 1], F32)
    p_sb = P.sb("p", [128, 256], F32)
    pn_sb = P.sb("pn", [128, 256], BF16)
    pT_sb = P.sb("pT", [128, 2, 128], BF16)

    for blk in range(NT // TB):
        c0 = blk * TB
        y_sb = u_sb
        yv = u_sb[:, 0:KYT, :].rearrange("p a b -> p (a b)")
        sv = u_sb[:, 32:32 + KYT, :].rearrange("p a b -> p (a b)")
        for seg in range(4):
            g0_ = seg * NT + c0
            hf_, col_ = g0_ // (S // 2), g0_ % (S // 2)
            ET = KYT // 4
            y5 = yall.rearrange("(hf et r p) n -> hf et r p n", hf=2, et=ET, r=4, p=128)
            for r_ in range(4):
                P.dma("sp", u_sb[:, 32 + r_ * ET:32 + (r_ + 1) * ET, :], y5[hf_, :, r_, :, col_:col_ + TB].rearrange("et p n -> p et n"), w=["u"])
            if seg == 0:
                P.op("dve", lambda e: e.tensor_scalar(out=yv, in0=sv, scalar1=sel[:, 0:1], scalar2=None, op0=ALU.mult), r=["u", "sel"], w=["u"])
            else:
                P.op("dve", lambda e, seg=seg: e.scalar_tensor_tensor(out=yv, in0=sv, scalar=sel[:, seg:seg + 1], in1=yv, op0=ALU.mult, op1=ALU.add),
                     r=["u", "sel"], w=["u"])
        P.dma("sp", x_sb[:], xT[:, c0:c0 + TB].rearrange("(t p) n -> p t n", p=128), w=["x"])
        if stage >= 2:
            T.linear(w_out, KY, 2048, y_sb, "u", T.evac_d(d_sb, "d"))
            import os
            if os.environ.get("NONR") != "1":
                T.normres(d_sb, "d", g_sb, 0, x_sb, "x")
        if stage < 3:
            P.dma("sp", xo[:, c0:c0 + TB].rearrange("(t p) n -> p t n", p=128), x_sb[:], r=["x"], w=["xo"])
            P.dma("sp", ho[:, c0:c0 + 256].rearrange("(t p) n -> p t n", p=128), h_sb[:, :, 0:256], r=["h"], w=["ho"])
            continue
        T.norm_to_bf16(x_sb, "x", g_sb, 1, h_sb, "h")

        def ev_q(mt, ps, pk):
            P.op("act", lambda e: e.activation(out=q_sb[:, mt, :], in_=ps[:], func=AF.Copy), x=[pk], w=["q"])
        T.linear(w_xq, 2048, 512, h_sb, "h", ev_q)
        for tt in range(4):
            for hd in range(4):
                P.op("pe", lambda e, tt=tt, hd=hd: e.matmul(T.at_ps[:, 0:256], lhsT=q_sb[:, hd, tt * 128:(tt + 1) * 128], rhs=kmT[:, hd, :],
                     start=True, stop=True), r=["q", "kmT"], w=["atps"])
                P.op("dve", lambda e: e.reduce_max(out=mx[:], in_=T.at_ps[:, 0:256], axis=AX.X), x=["atps"], w=["mx"])
                P.op("dve", lambda e: e.tensor_scalar(out=mx[:], in0=mx[:], scalar1=-SC, scalar2=None, op0=ALU.mult), r=["mx"], w=["mx"])
                P.op("act", lambda e: e.activation(out=p_sb[:], in_=T.at_ps[:, 0:256], func=AF.Exp, bias=mx[:], scale=SC, accum_out=rs[:]),
                     r=["mx"], x=["atps"], w=["p", "rs"])
                P.op("dve", lambda e: e.reciprocal(out=rs[:], in_=rs[:]), r=["rs"], w=["rs"])
                P.op("dve", lambda e: e.tensor_scalar(out=pn_sb[:], in0=p_sb[:], scalar1=rs[:, 0:1], scalar2=None, op0=ALU.mult),
                     r=["p", "rs"], w=["pn"])
                for mt in range(2):
                    P.op("pe", lambda e, mt=mt: e.transpose(out=tp_ps[:, mt * 128:(mt + 1) * 128], in_=pn_sb[:, mt * 128:(mt + 1) * 128], identity=T.ident[:]),
                         r=["pn", "ident"], w=["atps"])
                P.op("act", lambda e: e.activation(out=pT_sb[:].rearrange("p a b -> p (a b)"), in_=tp_ps[:, 0:256], func=AF.Copy), x=["atps"], w=["pT"])
                for mt in range(2):
                    P.op("pe", lambda e, mt=mt, hd=hd: e.matmul(T.at_ps[:, 256:384], lhsT=vm[:, mt, hd * 128:(hd + 1) * 128], rhs=pT_sb[:, mt, :],
                         start=(mt == 0), stop=(mt == 1)), r=["vm", "pT"], w=["atps"], sig=(mt == 1))
                P.op("act", lambda e, tt=tt, hd=hd: e.activation(out=o_sb[:, hd, tt * 128:(tt + 1) * 128], in_=T.at_ps[:, 256:384], func=AF.Copy),
                     x=["atps"], w=["o"])
        T.linear(w_xo, 512, 2048, o_sb, "o", T.evac_d(d_sb, "d"))
        T.normres(d_sb, "d", g_sb, 2, x_sb, "x")
        T.norm_to_bf16(x_sb, "x", g_sb, 3, h_sb, "h")

        def ev_u(mt, ps, pk):
            b = T.tmpc % 2
            T.tmpc += 1
            t = T.tmp[b]
            P.op("act", lambda e: e.activation(out=t[:], in_=ps[:], func=AF.Relu), x=[pk], w=["tmp%d" % b])
            P.op("dve", lambda e: e.tensor_tensor(out=u_sb[:, mt, :], in0=t[:], in1=t[:], op=ALU.mult), r=["tmp%d" % b], w=["u"])
        T.linear(w_up, 2048, 8192, h_sb, "h", ev_u)
        T.linear(w_down, 8192, 2048, u_sb, "u", T.evac_d(d_sb, "d"))
        T.normres(d_sb, "d", g_sb, 4, x_sb, "x")
        P.dma("sp", xo[:, c0:c0 + TB].rearrange("(t p) n -> p t n", p=128), x_sb[:], r=["x"], w=["xo"])
        if not last:
            T.norm_to_bf16(x_sb, "x", g_sb, 5, h_sb, "h")
            P.dma("sp", ho[:, c0:c0 + TB].rearrange("(t p) n -> p t n", p=128), h_sb[:], r=["h"], w=["ho"])


def phase_a0(P, nc, NT, D):
    xT, gT, ho = D["xT"], D["gT"], D["ho"]
    T = TL(P, nc)
    g_sb = P.sb("g", [128, 8, 16], F32)
    P.dma("sp", g_sb[:], gT, w=["g"])
    x_sb = P.sb("x", [128, 16, TB], F32)
    h_sb = P.sb("h", [128, 16, TB], BF16)
    for blk in range(NT // TB):
        c0 = blk * TB
        P.dma("sp", x_sb[:], xT[:, c0:c0 + TB].rearrange("(t p) n -> p t n", p=128), w=["x"])
        T.norm_to_bf16(x_sb, "x", g_sb, 0, h_sb, "h")
        P.dma("sp", ho[:, c0:c0 + TB].rearrange("(t p) n -> p t n", p=128), h_sb[:], r=["h"], w=["ho"])


def phase_ml(P, nc, S, D, hsrc):
    w_fm, w_tm, bgd, hgd, trid, nmd, idbd, yT = [D[k] for k in ("w_fm", "w_tm", "bg", "hg", "tri", "negmask", "idb", "yT")]
    wfm = P.sb("wfm", [128, 16, 512], BF16)
    wtm = P.sb("wtm", [128, 16, 1282], BF16)
    for k0 in range(0, 16, 4):
        P.dma("pool", wfm[:, k0:k0 + 4, :], w_fm[k0 * 128:(k0 + 4) * 128, :].rearrange("(k p) m -> p k m", p=128), w=["wfm"])
        P.dma("pool", wtm[:, k0:k0 + 4, :], w_tm[k0 * 128:(k0 + 4) * 128, :].rearrange("(k p) m -> p k m", p=128), w=["wtm"])
    b15 = P.sb("b15", [128, 2], F32)
    hg = P.sb("hg_sb", [128, 512], F32)
    tri = P.sb("tri_sb", [128, 128], F32)
    nm = P.sb("nm", [128, 128], F32)
    onesf = P.sb("onesf", [128, 128], F32)
    P.dma("sp", b15[:], bgd, w=["b15"])
    P.dma("sp", hg[:], hgd, w=["hg"])
    P.dma("sp", tri[:], trid, w=["tri"])
    P.dma("sp", nm[:], nmd, w=["nm"])
    idb = P.sb("idb", [128, 128], BF16)
    P.dma("sp", idb[:], idbd, w=["idb"])
    tpy = P.ps("tpy", [128, 512], BF16)
    yTs = P.sb("yTs", [128, 4, 128], BF16)
    P.op("dve", lambda e: e.memset(onesf[:], 1.0), w=["onesf"])
    P.op("dve", lambda e: e.tensor_scalar(out=b15[:], in0=b15[:], scalar1=1.0 / 15.0, scalar2=None, op0=ALU.mult), r=["b15"], w=["b15"])
    h_sb = [P.sb("h%d" % i, [128, 16, 512], BF16) for i in range(2)]
    qT = P.sb("qT", [128, 2, 512], BF16)
    kT = P.sb("kT", [128, 2, 512], BF16)
    vaug = P.sb("vaug", [128, 4, 513], BF16)
    gsig = P.sb("gsig", [128, 4, 512], F32)
    ktok = P.sb("ktok", [128, 4, 256], BF16)
    graw = P.sb("graw", [128, 4, 2], F32)
    Cf = P.sb("Cf", [128, 2, 513], F32)
    Cb = P.sb("Cb", [128, 2, 513], BF16)
    P.op("dve", lambda e: e.memset(Cf[:], 0.0), w=["Cf"])
    P.op("dve", lambda e: e.memset(Cb[:], 0.0), w=["Cb"])
    P.op("dve", lambda e: e.memset(vaug[:], 1.0), w=["vaug"])
    pf = P.ps("pf", [128, 512], F32)
    pt = P.ps("pt", [128, 512], F32)
    misc = P.ps("misc", [128, 512], F32)
    outp = P.ps("outp", [128, 512], F32)
    dC = [P.ps("dC%d" % j, [128, 512], F32) for j in range(2)]
    sm = lambda n, c=1: P.sb(n, [128, c], F32)
    th, ee, ll, cs, bend, wexp, ast, den, rr, ss, rstd = [sm(n, 2 if n in ("th", "ee", "ll") else 1) for n in
                                                          ("th", "ee", "ll", "cs", "bend", "wexp", "ast", "den", "rr", "ss", "rstd")]
    LF = P.sb("LF", [128, 128], F32)
    arg = P.sb("arg", [128, 128], F32)
    DT = P.sb("DT", [128, 128], F32)
    Arow = P.sb("Arow", [128, 128], F32)
    pmT = P.sb("pmT", [128, 128], BF16)
    qs = P.sb("qs", [128, 2, 128], BF16)
    hcs = P.sb("hcs", [128, 512], F32)
    junk = P.sb("junk", [128, 512], F32)
    ysb = P.sb("ysb", [128, 512], BF16)
    vw = P.sb("vw", [128, 513], BF16)
    sgt = P.sb("sgt", [128, 512], F32)

    for blk in range(S // 512):
        hb = h_sb[blk % 2]
        hk = "h%d" % (blk % 2)
        c0 = blk * 512
        for hh_ in range(2):
            P.dma("sp", hb[:].rearrange("p (j h) n -> p j h n", h=2)[:, :, hh_, :], hsrc(c0, hh_), w=[hk])
        for mt in range(4):
            for kt in range(16):
                P.op("pe", lambda e, mt=mt, kt=kt, hb=hb: e.matmul(pf[:], lhsT=wfm[:, kt, mt * 128:(mt + 1) * 128], rhs=hb[:, kt, :],
                     start=(kt == 0), stop=(kt == 15)), r=["wfm", hk], w=["pf"], sig=(kt == 15))
            if mt < 2:
                P.op("act", lambda e, mt=mt: e.activation(out=qT[:, mt, :], in_=pf[:], func=AF.Copy, scale=1.0 / 16.0), x=["pf"], w=["qT"])
            else:
                P.op("act", lambda e, mt=mt: e.activation(out=kT[:, mt - 2, :], in_=pf[:], func=AF.Copy), x=["pf"], w=["kT"])
        for tt in range(4):
            for (g0, gn) in ((0, 512), (512, 512), (1024, 258)):
                for kt in range(16):
                    P.op("pe", lambda e, tt=tt, kt=kt, g0=g0, gn=gn, hb=hb: e.matmul(pt[:, 0:gn], lhsT=hb[:, kt, tt * 128:(tt + 1) * 128],
                         rhs=wtm[:, kt, g0:g0 + gn], start=(kt == 0), stop=(kt == 15)), r=["wtm", hk], w=["pt"], sig=(kt == 15))
                if g0 == 0:
                    P.op("act", lambda e, tt=tt: e.activation(out=vaug[:, tt, 0:512], in_=pt[:], func=AF.Copy), x=["pt"], w=["vaug"])
                elif g0 == 512:
                    P.op("act", lambda e, tt=tt: e.activation(out=sgt[:], in_=pt[:], func=AF.Sigmoid), x=["pt"], w=["sgt"])
                    P.op("dve", lambda e, tt=tt: e.tensor_tensor(out=gsig[:, tt, :], in0=sgt[:], in1=hg[:], op=ALU.mult), r=["sgt", "hg"], w=["gsig"])
                else:
                    P.op("act", lambda e, tt=tt: e.activation(out=ktok[:, tt, :], in_=pt[:, 0:256], func=AF.Copy), x=["pt"], w=["ktok"])
                    P.op("dve", lambda e, tt=tt: e.tensor_copy(out=graw[:, tt, :], in_=pt[:, 256:258]), x=["pt"], w=["graw"])
        for tt in range(4):
            cc = slice(tt * 128, (tt + 1) * 128)
            for j in range(2):
                P.op("act", lambda e, j=j, tt=tt: e.activation(out=th[:, j:j + 1], in_=graw[:, tt, j:j + 1], func=AF.Tanh, bias=b15[:, j:j + 1], scale=1.0 / 15.0),
                     r=["graw", "b15"], w=["th"])
            P.op("act", lambda e: e.activation(out=ee[:, 0:1], in_=th[:, 1:2], func=AF.Exp, scale=-15.0), r=["th"], w=["ee"])
            P.op("act", lambda e: e.activation(out=ll[:, 0:1], in_=ee[:, 0:1], func=AF.Ln, bias=1.0), r=["ee"], w=["ll"])
            P.op("dve", lambda e: e.tensor_scalar(out=ll[:, 1:2], in0=ll[:, 0:1], scalar1=-1.0, scalar2=None, op0=ALU.mult), r=["ll"], w=["ll"])
            P.op("dve", lambda e: e.tensor_scalar(out=LF[:], in0=onesf[:], scalar1=ll[:, 1:2], scalar2=None, op0=ALU.mult), r=["ll", "onesf"], w=["LF"])
            P.op("pe", lambda e: e.matmul(misc[:, 0:128], lhsT=LF[:], rhs=tri[:], start=True, stop=True), r=["LF", "tri"], w=["misc"])
            P.op("pe", lambda e: e.matmul(misc[:, 257:258], lhsT=tri[:], rhs=ll[:, 1:2], start=True, stop=True), r=["ll", "tri"], w=["misc"])
            for j in range(2):
                P.op("pe", lambda e, j=j, cc=cc: e.matmul(misc[:, 128:256], lhsT=kT[:, j, cc], rhs=qT[:, j, cc], start=(j == 0), stop=(j == 1)),
                     r=["kT", "qT"], w=["misc"], sig=(j == 1))
            P.op("dve", lambda e: e.scalar_tensor_tensor(out=cs[:], in0=th[:, 0:1], scalar=15.0, in1=misc[:, 257:258], op0=ALU.mult, op1=ALU.subtract),
                 r=["th"], x=["misc"], w=["cs"])
            P.op("dve", lambda e: e.scalar_tensor_tensor(out=arg[:], in0=misc[:, 0:128], scalar=cs[:, 0:1], in1=nm[:], op0=ALU.add, op1=ALU.add),
                 r=["cs", "nm"], x=["misc"], w=["arg"])
            P.op("dve", lambda e: e.tensor_copy(out=bend[:], in_=misc[:, 127:128]), x=["misc"], w=["bend"])
            P.op("act", lambda e: e.activation(out=Arow[:], in_=misc[:, 0:128], func=AF.Exp), x=["misc"], w=["Arow"])
            P.op("act", lambda e: e.activation(out=DT[:], in_=arg[:], func=AF.Exp), r=["arg"], w=["DT"])
            P.op("act", lambda e: e.activation(out=wexp[:], in_=cs[:], func=AF.Exp, bias=bend[:, 0:1]), r=["cs", "bend"], w=["wexp"])
            P.op("act", lambda e: e.activation(out=ast[:], in_=bend[:], func=AF.Exp), r=["bend"], w=["ast"])
            P.op("dve", lambda e: e.tensor_tensor(out=pmT[:], in0=misc[:, 128:256], in1=DT[:], op=ALU.mult), r=["DT"], x=["misc"], w=["pmT"])
            for j in range(2):
                P.op("dve", lambda e, j=j, cc=cc: e.tensor_tensor(out=qs[:, j, :], in0=qT[:, j, cc], in1=Arow[:], op=ALU.mult), r=["qT", "Arow"], w=["qs"])
            for j in range(2):
                P.op("pe", lambda e, j=j: e.matmul(outp[:], lhsT=qs[:, j, :], rhs=Cb[:, j, 0:512], start=(j == 0), stop=False), r=["qs", "Cb"], w=["outp"], sig=False)
            P.op("pe", lambda e, tt=tt: e.matmul(outp[:], lhsT=pmT[:], rhs=vaug[:, tt, 0:512], start=False, stop=True), r=["pmT", "vaug"], w=["outp"])
            for j in range(2):
                P.op("pe", lambda e, j=j: e.matmul(misc[:, 256:257], lhsT=qs[:, j, :], rhs=Cb[:, j, 512:513], start=(j == 0), stop=False), r=["qs", "Cb"], w=["misc"], sig=False)
            P.op("pe", lambda e, tt=tt: e.matmul(misc[:, 256:257], lhsT=pmT[:], rhs=vaug[:, tt, 512:513], start=False, stop=True), r=["pmT", "vaug"], w=["misc"])
            P.op("act", lambda e: e.activation(out=rr[:], in_=misc[:, 256:257], func=AF.Abs), x=["misc"], w=["rr"])
            P.op("dve", lambda e: e.tensor_scalar(out=rr[:], in0=rr[:], scalar1=1.0, scalar2=None, op0=ALU.max), r=["rr"], w=["rr"])
            P.op("dve", lambda e: e.reciprocal(out=rr[:], in_=rr[:]), r=["rr"], w=["rr"])
            P.op("act", lambda e: e.activation(out=hcs[:], in_=outp[:], func=AF.Copy, scale=rr[:, 0:1]), r=["rr"], x=["outp"], w=["hcs"])
            P.op("act", lambda e: e.activation(out=junk[:], in_=hcs[:], func=AF.Square, accum_out=ss[:]), r=["hcs"], w=["junk", "ss"])
            P.op("dve", lambda e: e.tensor_scalar(out=rstd[:], in0=ss[:], scalar1=1.0 / 512.0, scalar2=EPS, op0=ALU.mult, op1=ALU.add), r=["ss"], w=["rstd"])
            P.op("act", lambda e: e.activation(out=rstd[:], in_=rstd[:], func=AF.Sqrt), r=["rstd"], w=["rstd"])
            P.op("dve", lambda e: e.reciprocal(out=rstd[:], in_=rstd[:]), r=["rstd"], w=["rstd"])
            P.op("dve", lambda e, tt=tt: e.scalar_tensor_tensor(out=ysb[:], in0=hcs[:], scalar=rstd[:, 0:1], in1=gsig[:, tt, :], op0=ALU.mult, op1=ALU.mult),
                 r=["hcs", "rstd", "gsig"], w=["ysb"])
            for j in range(4):
                P.op("pe", lambda e, j=j: e.transpose(out=tpy[:, j * 128:(j + 1) * 128], in_=ysb[:, j * 128:(j + 1) * 128], identity=idb[:]), r=["ysb", "idb"], w=["tpy"])
            P.op("act", lambda e: e.activation(out=yTs[:].rearrange("p a b -> p (a b)"), in_=tpy[:, 0:512], func=AF.Copy), x=["tpy"], w=["yTs"])
            t0 = c0 + tt * 128
            hf_, col_ = t0 // (S // 2), t0 % (S // 2)
            P.dma("sp", yT[hf_ * 512:(hf_ + 1) * 512, col_:col_ + 128].rearrange("(j p) n -> p j n", p=128), yTs[:], r=["yTs"], w=["yT"])
            P.op("dve", lambda e, tt=tt: e.tensor_scalar(out=vw[:], in0=vaug[:, tt, :], scalar1=wexp[:, 0:1], scalar2=None, op0=ALU.mult), r=["vaug", "wexp"], w=["vw"])
            for j in range(2):
                P.op("pe", lambda e, j=j, tt=tt: e.matmul(dC[j][:], lhsT=ktok[:, tt, j * 128:(j + 1) * 128], rhs=vw[:, 0:512], start=True, stop=True),
                     r=["ktok", "vw"], w=["dC%d" % j])
                P.op("pe", lambda e, j=j, tt=tt: e.matmul(misc[:, 258 + j:259 + j], lhsT=ktok[:, tt, j * 128:(j + 1) * 128], rhs=vw[:, 512:513], start=True, stop=True),
                     r=["ktok", "vw"], w=["misc"])
            for j in range(2):
                P.op("dve", lambda e, j=j: e.scalar_tensor_tensor(out=Cf[:, j, 0:512], in0=Cf[:, j, 0:512], scalar=ast[:, 0:1], in1=dC[j][:], op0=ALU.mult, op1=ALU.add),
                     r=["Cf", "ast"], x=["dC%d" % j], w=["Cf"])
                P.op("dve", lambda e, j=j: e.scalar_tensor_tensor(out=Cf[:, j, 512:513], in0=Cf[:, j, 512:513], scalar=ast[:, 0:1], in1=misc[:, 258 + j:259 + j], op0=ALU.mult, op1=ALU.add),
                     r=["Cf", "ast"], x=["misc"], w=["Cf"])
                P.op("act", lambda e, j=j: e.activation(out=Cb[:, j, :], in_=Cf[:, j, :], func=AF.Copy), r=["Cf"], w=["Cb"])


def phase_gd(P, nc, S, D, hsrc):
    w_fm, w_tm, cwd, alogd, dtbd, ngd, trid, nmd, mltd, moffd, idfd, idbd, yT = [D[k] for k in
        ("w_fm", "w_tm", "cw", "alog", "dtb", "ng", "tri", "nmle", "mlt", "moff", "idf", "idb", "yT")]
    wfm = P.sb("wfm", [128, 16, 2048], BF16)
    wtm = P.sb("wtm", [128, 16, 1040], BF16)
    for k0 in range(0, 16, 2):
        P.dma("pool", wfm[:, k0:k0 + 2, :], w_fm[k0 * 128:(k0 + 2) * 128, :].rearrange("(k p) m -> p k m", p=128), w=["wfm"])
        P.dma("pool", wtm[:, k0:k0 + 2, :], w_tm[k0 * 128:(k0 + 2) * 128, :].rearrange("(k p) m -> p k m", p=128), w=["wtm"])
    cst = {}
    for n, d, shp, dt_ in (("cw", cwd, [128, 16, 4], F32), ("alog", alogd, [128, 8], F32), ("dtb", dtbd, [128, 8], F32), ("ng", ngd, [128, 128], F32),
                           ("tri", trid, [128, 128], F32), ("nmle", nmd, [128, 128], F32), ("mlt", mltd, [128, 128], F32), ("moff", moffd, [128, 128], F32),
                           ("idf", idfd, [128, 128], F32), ("idb", idbd, [128, 128], BF16)):
        cst[n] = P.sb(n + "_sb", shp, dt_)
        P.dma("sp", cst[n][:], d, w=[n])
    cw, alog, dtb, ng, tri, nmle, mlt, idf, idb = [cst[n] for n in ("cw", "alog", "dtb", "ng", "tri", "nmle", "mlt", "idf", "idb")]
    moff = cst["moff"]
    onesf = P.sb("onesf", [128, 128], F32)
    onesb = P.sb("onesb", [128, 128], BF16)
    P.op("dve", lambda e: e.memset(onesf[:], 1.0), w=["onesf"])
    P.op("dve", lambda e: e.memset(onesb[:], 1.0), w=["onesb"])
    nea = P.sb("nea", [128, 8], F32)
    P.op("act", lambda e: e.activation(out=nea[:], in_=alog[:], func=AF.Exp), r=["alog"], w=["nea"])
    P.op("dve", lambda e: e.tensor_scalar(out=nea[:], in0=nea[:], scalar1=-1.0, scalar2=None, op0=ALU.mult), r=["nea"], w=["nea"])
    hb = P.sb("hb", [128, 16, 512], BF16)
    pre1 = P.sb("pre", [128, 515], F32)
    hist = P.sb("hist", [128, 16, 3], F32)
    P.op("dve", lambda e: e.memset(hist[:], 0.0), w=["hist"])
    acc = P.sb("acc", [128, 512], F32)
    cvo = P.sb("cvo", [128, 512], F32)
    sqb = P.sb("sqb", [128, 512], BF16)
    rn = P.sb("rn", [128, 512], F32)
    qT = P.sb("qT", [128, 4, 512], BF16)
    kT = P.sb("kT", [128, 4, 512], BF16)
    vT = P.sb("vT", [128, 8, 512], BF16)
    vtok = P.sb("vtok", [128, 4, 1024], BF16)
    ktok = P.sb("ktok", [128, 4, 512], BF16)
    zs = P.sb("zs", [128, 4, 1024], BF16)
    beta = P.sb("beta", [128, 4, 8], F32)
    gg = P.sb("gg", [128, 4, 8], F32)
    gcol = P.sb("gcol", [128, 4, 8], F32)
    Sf = P.sb("Sf", [128, 8, 128], F32)
    Sb = P.sb("Sb", [128, 8, 128], BF16)
    P.op("dve", lambda e: e.memset(Sf[:], 0.0), w=["Sf"])
    P.op("dve", lambda e: e.memset(Sb[:], 0.0), w=["Sb"])
    pf = P.ps("pf", [128, 512], F32)
    pt = P.ps("pt", [128, 512], F32)
    misc = P.ps("misc", [128, 512], F32)
    kk = P.ps("kk", [128, 512], F32)
    sol = P.ps("sol", [128, 512], F32)
    sqp = P.ps("sqp", [128, 512], F32)
    rec = P.ps("rec", [128, 512], F32)
    tpp = P.ps("tpp", [128, 1024], BF16)
    sm = lambda n: P.sb(n, [128, 1], F32)
    egc, glast, el, ekd, ss, rstd = [sm(n) for n in ("egc", "glast", "el", "ekd", "ss", "rstd")]
    LF = P.sb("LF", [128, 128], F32)
    arg = P.sb("arg", [128, 128], F32)
    E1 = P.sb("E1", [128, 128], F32)
    E2 = P.sb("E2", [128, 128], F32)
    Erow = P.sb("Erow", [128, 128], F32)
    AqkT = P.sb("AqkT", [128, 128], BF16)
    PT = [P.sb("PT%d" % i, [128, 128], F32) for i in range(6)]
    AoT = P.sb("AoT", [128, 128], F32)
    TT = P.sb("TT", [128, 128], F32)
    Zb = P.sb("Zb", [128, 256], F32)
    Pm = [P.sb("Pm%d" % i, [128, 128], F32) for i in range(2)]
    X = P.sb("X", [128, 256], F32)
    Ub = P.sb("Ub", [128, 128], F32)
    Wb = P.sb("Wb", [128, 128], F32)
    WT = P.sb("WT", [128, 128], F32)
    vn = P.sb("vn", [128, 128], BF16)
    qg = P.sb("qg", [128, 128], BF16)
    kd = P.sb("kd", [128, 128], BF16)
    of = P.sb("of", [128, 128], F32)
    junk = P.sb("junk", [128, 128], F32)
    ysb = P.sb("ysb", [128, 1024], BF16)
    yTs = P.sb("yTs", [128, 8, 128], BF16)
    tmpz = P.sb("tmpz", [128, 16], F32)

    for blk in range(S // 512):
        c0 = blk * 512
        for hh_ in range(2):
            P.dma("sp", hb[:].rearrange("p (j h) n -> p j h n", h=2)[:, :, hh_, :], hsrc(c0, hh_), w=["hb"])
        for mt in range(16):
            for kt in range(16):
                P.op("pe", lambda e, mt=mt, kt=kt: e.matmul(pf[:], lhsT=wfm[:, kt, mt * 128:(mt + 1) * 128], rhs=hb[:, kt, :],
                     start=(kt == 0), stop=(kt == 15)), r=["wfm", "hb"], w=["pf"], sig=(kt == 15))
            P.op("dve", lambda e, mt=mt: e.tensor_copy(out=pre1[:, 0:3], in_=hist[:, mt, :]), r=["hist"], w=["pre"])
            P.op("act", lambda e, mt=mt: e.activation(out=pre1[:, 3:515], in_=pf[:], func=AF.Copy), x=["pf"], w=["pre"])
            P.op("dve", lambda e, mt=mt: e.tensor_scalar(out=acc[:], in0=pre1[:, 0:512], scalar1=cw[:, mt, 0:1], scalar2=None, op0=ALU.mult),
                 r=["pre", "cw"], w=["acc"])
            for j in range(1, 4):
                P.op("dve", lambda e, mt=mt, j=j: e.scalar_tensor_tensor(out=acc[:], in0=pre1[:, j:j + 512], scalar=cw[:, mt, j:j + 1], in1=acc[:],
                     op0=ALU.mult, op1=ALU.add), r=["pre", "cw", "acc"], w=["acc"])
            P.op("dve", lambda e, mt=mt: e.tensor_copy(out=hist[:, mt, :], in_=pre1[:, 512:515]), r=["pre"], w=["hist"])
            if mt >= 8:
                P.op("act", lambda e, mt=mt: e.activation(out=vT[:, mt - 8, :], in_=acc[:], func=AF.Silu), r=["acc"], w=["vT"])
            else:
                P.op("act", lambda e: e.activation(out=cvo[:], in_=acc[:], func=AF.Silu), r=["acc"], w=["cvo"])
                P.op("act", lambda e: e.activation(out=sqb[:], in_=cvo[:], func=AF.Square), r=["cvo"], w=["sqb"])
                P.op("pe", lambda e: e.matmul(pt[:], lhsT=onesb[:], rhs=sqb[:], start=True, stop=True), r=["onesb", "sqb"], w=["pt"])
                P.op("dve", lambda e: e.tensor_scalar(out=rn[:], in0=pt[:], scalar1=EPS, scalar2=None, op0=ALU.add), x=["pt"], w=["rn"])
                P.op("act", lambda e: e.activation(out=rn[:], in_=rn[:], func=AF.Sqrt), r=["rn"], w=["rn"])
                P.op("dve", lambda e: e.reciprocal(out=rn[:], in_=rn[:]), r=["rn"], w=["rn"])
                if mt < 4:
                    P.op("dve", lambda e, mt=mt: e.scalar_tensor_tensor(out=qT[:, mt, :], in0=cvo[:], scalar=128 ** -0.5, in1=rn[:], op0=ALU.mult, op1=ALU.mult),
                         r=["cvo", "rn"], w=["qT"])
                else:
                    P.op("dve", lambda e, mt=mt: e.tensor_tensor(out=kT[:, mt - 4, :], in0=cvo[:], in1=rn[:], op=ALU.mult), r=["cvo", "rn"], w=["kT"])
        for tt in range(4):
            cc = slice(tt * 128, (tt + 1) * 128)
            for i in range(4):
                P.op("pe", lambda e, i=i, cc=cc: e.transpose(out=tpp[:, i * 128:(i + 1) * 128], in_=kT[:, i, cc], identity=idb[:]), r=["kT", "idb"], w=["tpp"])
            P.op("act", lambda e, tt=tt: e.activation(out=ktok[:, tt, :], in_=tpp[:, 0:512], func=AF.Copy), x=["tpp"], w=["ktok"])
            for i in range(8):
                P.op("pe", lambda e, i=i, cc=cc: e.transpose(out=tpp[:, i * 128:(i + 1) * 128], in_=vT[:, i, cc], identity=idb[:]), r=["vT", "idb"], w=["tpp"])
            P.op("act", lambda e, tt=tt: e.activation(out=vtok[:, tt, :], in_=tpp[:, 0:1024], func=AF.Copy), x=["tpp"], w=["vtok"])
        for tt in range(4):
            for (g0, gn) in ((0, 512), (512, 512), (1024, 16)):
                for kt in range(16):
                    P.op("pe", lambda e, tt=tt, kt=kt, g0=g0, gn=gn: e.matmul(pt[:, 0:gn], lhsT=hb[:, kt, tt * 128:(tt + 1) * 128],
                         rhs=wtm[:, kt, g0:g0 + gn], start=(kt == 0), stop=(kt == 15)), r=["wtm", "hb"], w=["pt"], sig=(kt == 15))
                if g0 < 1024:
                    P.op("act", lambda e, tt=tt, g0=g0: e.activation(out=zs[:, tt, g0:g0 + 512], in_=pt[:], func=AF.Silu), x=["pt"], w=["zs"])
                else:
                    P.op("act", lambda e, tt=tt: e.activation(out=beta[:, tt, :], in_=pt[:, 0:8], func=AF.Sigmoid), x=["pt"], w=["beta"])
                    P.op("dve", lambda e: e.tensor_tensor(out=tmpz[:, 0:8], in0=pt[:, 8:16], in1=dtb[:], op=ALU.add), r=["dtb"], x=["pt"], w=["tmpz"])
                    P.op("act", lambda e: e.activation(out=tmpz[:, 0:8], in_=tmpz[:, 0:8], func=AF.Exp), r=["tmpz"], w=["tmpz"])
                    P.op("act", lambda e: e.activation(out=tmpz[:, 8:16], in_=tmpz[:, 0:8], func=AF.Ln, bias=1.0), r=["tmpz"], w=["tmpz"])
                    P.op("dve", lambda e, tt=tt: e.tensor_tensor(out=gg[:, tt, :], in0=tmpz[:, 8:16], in1=nea[:], op=ALU.mult), r=["tmpz", "nea"], w=["gg"])
            P.op("pe", lambda e, tt=tt: e.matmul(misc[:, 256:264], lhsT=tri[:], rhs=gg[:, tt, :], start=True, stop=True), r=["tri", "gg"], w=["misc"])
            P.op("dve", lambda e, tt=tt: e.tensor_copy(out=gcol[:, tt, :], in_=misc[:, 256:264]), x=["misc"], w=["gcol"])
        for tt in range(4):
            cc = slice(tt * 128, (tt + 1) * 128)
            for h in range(8):
                kh = h // 2
                gc = gcol[:, tt, h:h + 1]
                bcol = beta[:, tt, h:h + 1]
                P.op("dve", lambda e, tt=tt, h=h: e.tensor_scalar(out=LF[:], in0=onesf[:], scalar1=gg[:, tt, h:h + 1], scalar2=None, op0=ALU.mult), r=["gg", "onesf"], w=["LF"])
                P.op("pe", lambda e: e.matmul(misc[:, 0:128], lhsT=LF[:], rhs=tri[:], start=True, stop=True), r=["LF", "tri"], w=["misc"])
                P.op("dve", lambda e, gc=gc: e.scalar_tensor_tensor(out=arg[:], in0=misc[:, 0:128], scalar=gc, in1=nmle[:], op0=ALU.subtract, op1=ALU.add),
                     r=["gcol", "nmle"], x=["misc"], w=["arg"])
                P.op("dve", lambda e: e.tensor_copy(out=glast[:], in_=misc[:, 127:128]), x=["misc"], w=["glast"])
                P.op("act", lambda e: e.activation(out=Erow[:], in_=misc[:, 0:128], func=AF.Exp), x=["misc"], w=["Erow"])
                P.op("act", lambda e: e.activation(out=E1[:], in_=arg[:], func=AF.Exp), r=["arg"], w=["E1"])
                P.op("act", lambda e, gc=gc: e.activation(out=egc[:], in_=gc, func=AF.Exp), r=["gcol"], w=["egc"])
                P.op("act", lambda e: e.activation(out=el[:], in_=glast[:], func=AF.Exp), r=["glast"], w=["el"])
                P.op("act", lambda e, gc=gc: e.activation(out=ekd[:], in_=gc, func=AF.Exp, scale=-1.0, bias=glast[:, 0:1]), r=["gcol", "glast"], w=["ekd"])
                P.op("dve", lambda e: e.tensor_tensor(out=E2[:], in0=E1[:], in1=mlt[:], op=ALU.mult), r=["E1", "mlt"], w=["E2"])
                P.op("pe", lambda e, kh=kh, cc=cc: e.matmul(kk[:, 0:128], lhsT=kT[:, kh, cc], rhs=kT[:, kh, cc], start=True, stop=True), r=["kT"], w=["kk"])
                P.op("pe", lambda e, kh=kh, cc=cc: e.matmul(kk[:, 128:256], lhsT=kT[:, kh, cc], rhs=qT[:, kh, cc], start=True, stop=True), r=["kT", "qT"], w=["kk"])
                P.op("dve", lambda e: e.tensor_tensor(out=AqkT[:], in0=kk[:, 128:256], in1=E1[:], op=ALU.mult), r=["E1"], x=["kk"], w=["AqkT"])
                P.op("dve", lambda e: e.tensor_tensor(out=PT[0][:], in0=kk[:, 0:128], in1=E2[:], op=ALU.mult), r=["E2"], x=["kk"], w=["PT0"])
                P.op("dve", lambda e, bcol=bcol: e.tensor_scalar(out=PT[0][:], in0=PT[0][:], scalar1=bcol, scalar2=None, op0=ALU.mult), r=["PT0", "beta"], w=["PT0"])
                P.op("dve", lambda e: e.tensor_tensor(out=AoT[:], in0=E1[:], in1=moff[:], op=ALU.mult), r=["E1", "moff"], w=["AoT"])
                P.op("dve", lambda e: e.tensor_tensor(out=AoT[:], in0=kk[:, 0:128], in1=AoT[:], op=ALU.mult), r=["AoT"], x=["kk"], w=["AoT"])
                P.op("dve", lambda e, bcol=bcol: e.tensor_scalar(out=AoT[:], in0=AoT[:], scalar1=bcol, scalar2=None, op0=ALU.mult), r=["AoT", "beta"], w=["AoT"])
                P.op("pe", lambda e: e.matmul(sqp[:, 0:128], lhsT=PT[0][:], rhs=idf[:], start=True, stop=True), r=["PT0", "idf"], w=["sqp"])
                P.op("act", lambda e: e.activation(out=Pm[0][:], in_=sqp[:, 0:128], func=AF.Copy), x=["sqp"], w=["Pm0"])
                P.op("dve", lambda e, tt=tt, h=h: e.tensor_copy(out=X[:, 0:128], in_=vtok[:, tt, h * 128:(h + 1) * 128]), r=["vtok"], w=["X"])
                P.op("dve", lambda e, tt=tt, kh=kh: e.tensor_scalar(out=X[:, 128:256], in0=ktok[:, tt, kh * 128:(kh + 1) * 128], scalar1=egc[:, 0:1], scalar2=None, op0=ALU.mult),
                     r=["ktok", "egc"], w=["X"])
                P.op("dve", lambda e: e.tensor_tensor(out=TT[:], in0=idf[:], in1=PT[0][:], op=ALU.subtract), r=["idf", "PT0"], w=["TT"])
                for j in range(5):
                    a, b = j % 2, (j + 1) % 2
                    P.op("pe", lambda e, j=j, a=a: e.matmul(sqp[:, 0:128], lhsT=PT[j][:], rhs=Pm[a][:], start=True, stop=True), r=["PT%d" % j, "Pm%d" % a], w=["sqp"])
                    P.op("pe", lambda e, j=j, a=a: e.matmul(sqp[:, 128:256], lhsT=Pm[a][:], rhs=PT[j][:], start=True, stop=True), r=["PT%d" % j, "Pm%d" % a], w=["sqp"])
                    P.op("act", lambda e, b=b: e.activation(out=Pm[b][:], in_=sqp[:, 0:128], func=AF.Copy), x=["sqp"], w=["Pm%d" % b])
                    P.op("act", lambda e, j=j: e.activation(out=PT[j + 1][:], in_=sqp[:, 128:256], func=AF.Copy), x=["sqp"], w=["PT%d" % (j + 1)])
                    P.op("pe", lambda e, b=b: e.matmul(sol[:, 0:128], lhsT=Pm[b][:], rhs=TT[:], start=True, stop=True), r=["Pm%d" % b, "TT"], w=["sol"])
                    P.op("dve", lambda e: e.tensor_tensor(out=TT[:], in0=TT[:], in1=sol[:, 0:128], op=ALU.add), r=["TT"], x=["sol"], w=["TT"])
                P.op("pe", lambda e: e.matmul(sol[:, 0:256], lhsT=TT[:], rhs=X[:], start=True, stop=True), r=["TT", "X"], w=["sol"])
                P.op("dve", lambda e: e.tensor_copy(out=X[:], in_=sol[:, 0:256]), x=["sol"], w=["X"])
                P.op("pe", lambda e: e.matmul(sol[:, 0:256], lhsT=AoT[:], rhs=X[:], start=True, stop=True), r=["AoT", "X"], w=["sol"])
                P.op("act", lambda e: e.activation(out=Zb[:], in_=sol[:, 0:256], func=AF.Copy), x=["sol"], w=["Zb"])
                P.op("pe", lambda e: e.matmul(sol[:, 0:256], lhsT=TT[:], rhs=Zb[:], start=True, stop=True), r=["TT", "Zb"], w=["sol"])
                P.op("dve", lambda e: e.tensor_tensor(out=X[:], in0=X[:], in1=sol[:, 0:256], op=ALU.subtract), r=["X"], x=["sol"], w=["X"])
                P.op("dve", lambda e, bcol=bcol: e.tensor_scalar(out=Ub[:], in0=X[:, 0:128], scalar1=bcol, scalar2=None, op0=ALU.mult), r=["X", "beta"], w=["Ub"])
                P.op("dve", lambda e, bcol=bcol: e.tensor_scalar(out=Wb[:], in0=X[:, 128:256], scalar1=bcol, scalar2=None, op0=ALU.mult), r=["X", "beta"], w=["Wb"])
                P.op("pe", lambda e: e.matmul(sol[:, 256:384], lhsT=Wb[:], rhs=idf[:], start=True, stop=True), r=["Wb", "idf"], w=["sol"])
                P.op("act", lambda e: e.activation(out=WT[:], in_=sol[:, 256:384], func=AF.Copy), x=["sol"], w=["WT"])
                P.op("pe", lambda e, h=h: e.matmul(rec[:, 0:128], lhsT=WT[:], rhs=Sf[:, h, :], start=True, stop=True), r=["WT", "Sf"], w=["rec"])
                P.op("dve", lambda e: e.tensor_tensor(out=vn[:], in0=Ub[:], in1=rec[:, 0:128], op=ALU.subtract), r=["Ub"], x=["rec"], w=["vn"])
                P.op("dve", lambda e, kh=kh, cc=cc: e.tensor_tensor(out=qg[:], in0=qT[:, kh, cc], in1=Erow[:], op=ALU.mult), r=["qT", "Erow"], w=["qg"])
                P.op("pe", lambda e, h=h: e.matmul(rec[:, 128:256], lhsT=qg[:], rhs=Sb[:, h, :], start=True, stop=False), r=["qg", "Sb"], w=["rec"], sig=False)
                P.op("pe", lambda e: e.matmul(rec[:, 128:256], lhsT=AqkT[:], rhs=vn[:], start=False, stop=True), r=["AqkT", "vn"], w=["rec"])
                P.op("dve", lambda e, tt=tt, kh=kh: e.tensor_scalar(out=kd[:], in0=ktok[:, tt, kh * 128:(kh + 1) * 128], scalar1=ekd[:, 0:1], scalar2=None, op0=ALU.mult),
                     r=["ktok", "ekd"], w=["kd"])
                P.op("pe", lambda e: e.matmul(rec[:, 256:384], lhsT=kd[:], rhs=vn[:], start=True, stop=True), r=["kd", "vn"], w=["rec"])
                P.op("dve", lambda e, h=h: e.scalar_tensor_tensor(out=Sf[:, h, :], in0=Sf[:, h, :], scalar=el[:, 0:1], in1=rec[:, 256:384], op0=ALU.mult, op1=ALU.add),
                     r=["Sf", "el"], x=["rec"], w=["Sf"])
                P.op("act", lambda e, h=h: e.activation(out=Sb[:, h, :], in_=Sf[:, h, :], func=AF.Copy), r=["Sf"], w=["Sb"])
                P.op("act", lambda e: e.activation(out=of[:], in_=rec[:, 128:256], func=AF.Copy), x=["rec"], w=["of"])
                P.op("act", lambda e: e.activation(out=junk[:], in_=of[:], func=AF.Square, accum_out=ss[:]), r=["of"], w=["junk", "ss"])
                P.op("dve", lambda e: e.tensor_scalar(out=rstd[:], in0=ss[:], scalar1=1.0 / 128.0, scalar2=EPS, op0=ALU.mult, op1=ALU.add), r=["ss"], w=["rstd"])
                P.op("act", lambda e: e.activation(out=rstd[:], in_=rstd[:], func=AF.Sqrt), r=["rstd"], w=["rstd"])
                P.op("dve", lambda e: e.reciprocal(out=rstd[:], in_=rstd[:]), r=["rstd"], w=["rstd"])
                P.op("dve", lambda e: e.scalar_tensor_tensor(out=of[:], in0=of[:], scalar=rstd[:, 0:1], in1=ng[:], op0=ALU.mult, op1=ALU.mult), r=["of", "rstd", "ng"], w=["of"])
                P.op("dve", lambda e, tt=tt, h=h: e.tensor_tensor(out=ysb[:, h * 128:(h + 1) * 128], in0=of[:], in1=zs[:, tt, h * 128:(h + 1) * 128], op=ALU.mult),
                     r=["of", "zs"], w=["ysb"])
            for j in range(8):
                P.op("pe", lambda e, j=j: e.transpose(out=tpp[:, j * 128:(j + 1) * 128], in_=ysb[:, j * 128:(j + 1) * 128], identity=idb[:]), r=["ysb", "idb"], w=["tpp"])
            P.op("act", lambda e: e.activation(out=yTs[:].rearrange("p a b -> p (a b)"), in_=tpp[:, 0:1024], func=AF.Copy), x=["tpp"], w=["yTs"])
            t0 = c0 + tt * 128
            hf_, col_ = t0 // (S // 2), t0 % (S // 2)
            P.dma("sp", yT[hf_ * 1024:(hf_ + 1) * 1024, col_:col_ + 128].rearrange("(j p) n -> p j n", p=128), yTs[:], r=["yTs"], w=["yT"])


def build_fused(S):
    NT = S // 4
    nc = bass.Bass("TRN2", target_bir_lowering=False)
    ext = lambda n, s, d, k="ExternalInput": nc.dram_tensor(n, list(s), d, kind=k).ap()
    itn = lambda n, s, d: nc.dram_tensor(n, list(s), d).ap()
    I = {}
    for n, shp, dt_ in (("xT", [2048, NT], F32), ("memT", [2048, 256], F32), ("gA", [128, 8, 16], F32), ("gT0", [128, 8, 16], F32),
                        ("gT1", [128, 8, 16], F32), ("sel", [128, 4], F32), ("identb", [128, 128], BF16), ("identf", [128, 128], F32),
                        ("tri", [128, 128], F32), ("nmle", [128, 128], F32), ("mlt", [128, 128], F32), ("moff", [128, 128], F32),
                        ("w_out0", [2048, 2048], F32), ("w_out1", [4096, 2048], F32), ("w_xq0", [2048, 512], F32), ("w_xq1", [2048, 512], F32),
                        ("w_xo0", [512, 2048], F32), ("w_xo1", [512, 2048], F32), ("w_up0", [2048, 8192], F32), ("w_up1", [2048, 8192], F32),
                        ("w_down0", [8192, 2048], F32), ("w_down1", [8192, 2048], F32), ("w_kv", [2048, 1024], F32),
                        ("ml_w_fm", [2048, 512], F32), ("ml_w_tm", [2048, 1282], F32), ("ml_bg", [128, 2], F32), ("ml_hg", [128, 512], F32),
                        ("gd_w_fm", [2048, 2048], F32), ("gd_w_tm", [2048, 1040], F32), ("gd_cw", [128, 16, 4], F32), ("gd_alog", [128, 8], F32),
                        ("gd_dtb", [128, 8], F32), ("gd_ng", [128, 128], F32)):
        I[n] = ext(n, shp, dt_)
    xo = ext("xo", [2048, NT], F32, "ExternalOutput")
    hloc = itn("hloc", [2048, NT], BF16)
    hall = itn("hall", [4 * 2048, NT], BF16)
    y0loc = itn("y0loc", [2 * 512, S // 2], BF16)
    y0all = itn("y0all", [2 * 2048, S // 2], BF16)
    y1loc = itn("y1loc", [2 * 1024, S // 2], BF16)
    y1all = itn("y1all", [2 * 4096, S // 2], BF16)
    xbuf = itn("xbuf", [2048, NT], F32)
    groups = [[0, 1, 2, 3], [4, 5, 6, 7]]
    P = Prog(nc)

    def hsrc(c0, hh):
        seg, lc = c0 // NT, c0 % NT
        return hall.rearrange("(j s h p) n -> j s h p n", j=8, s=4, h=2, p=128)[:, seg, hh, :, lc:lc + 512].rearrange("j p n -> p j n")

    def gather(loc, allb, rows, kr, kw):
        for j in range(loc.shape[0] // rows):
            P.coll("AllGather", loc[j * rows:(j + 1) * rows, :], allb[j * 4 * rows:(j + 1) * 4 * rows, :], groups, r=[kr], w=[kw])

    phase_a0(P, nc, NT, dict(xT=I["xT"], gT=I["gA"], ho=hloc))
    P.end_phase()
    gather(hloc, hall, 256, "ho", "hall")
    P.barrier()
    phase_ml(P, nc, S, dict(w_fm=I["ml_w_fm"], w_tm=I["ml_w_tm"], bg=I["ml_bg"], hg=I["ml_hg"], tri=I["tri"], negmask=I["nmle"],
                            idb=I["identb"], yT=y0loc), hsrc)
    P.end_phase()
    gather(y0loc, y0all, 128, "yT", "y0all")
    P.barrier()
    phase_tl(P, nc, NT, 2048, False, S=S, D=dict(yall=y0all, sel=I["sel"], xT=I["xT"], memT=I["memT"], gT=I["gT0"], w_out=I["w_out0"], w_xq=I["w_xq0"],
                                          w_xo=I["w_xo0"], w_up=I["w_up0"], w_down=I["w_down0"], w_kv=I["w_kv"], identd=I["identb"], xo=xbuf, ho=hloc))
    P.end_phase()
    gather(hloc, hall, 256, "ho", "hall")
    P.barrier()
    phase_gd(P, nc, S, dict(w_fm=I["gd_w_fm"], w_tm=I["gd_w_tm"], cw=I["gd_cw"], alog=I["gd_alog"], dtb=I["gd_dtb"], ng=I["gd_ng"], tri=I["tri"],
                            nmle=I["nmle"], mlt=I["mlt"], moff=I["moff"], idf=I["identf"], idb=I["identb"], yT=y1loc), hsrc)
    P.end_phase()
    gather(y1loc, y1all, 128, "yT", "y1all")
    P.barrier()
    phase_tl(P, nc, NT, 4096, True, S=S, D=dict(yall=y1all, sel=I["sel"], xT=xbuf, memT=I["memT"], gT=I["gT1"], w_out=I["w_out1"], w_xq=I["w_xq1"],
                                         w_xo=I["w_xo1"], w_up=I["w_up1"], w_down=I["w_down1"], w_kv=I["w_kv"], identd=I["identb"], xo=xo))
    P.wait_all("sp", ["xo"])
    P.emit()
    P.close()
    return nc


def _gT(rows):
    rows = list(rows) + [np.zeros(2048, np.float32)] * (8 - len(rows))
    return np.ascontiguousarray(np.stack(rows).astype(np.float32).reshape(8, 16, 128).transpose(2, 0, 1))


def kernel(x, mem, norm_g, mem_norm_g, w_mem_kv, w_xq, w_xo, w_up, w_down,
           mlstm_w_in, mlstm_b_gates, mlstm_head_g, mlstm_w_out,
           gdn_w_in, gdn_conv_w, gdn_a_log, gdn_dt_bias, gdn_norm_g, gdn_w_out):
    bf = ml_dtypes.bfloat16
    f = lambda a: np.ascontiguousarray(np.asarray(a, dtype=np.float32))
    x, mem, norm_g, mem_norm_g = f(x), f(mem), f(norm_g), f(mem_norm_g)
    B, S, D = x.shape
    NT = S // 4
    cores = list(range(8))
    eye = np.eye(128, dtype=np.float32)
    tri = np.triu(np.ones((128, 128), np.float32))
    nmle = np.where(tri > 0, 0.0, -30000.0).astype(np.float32)
    _ii = np.arange(128)
    mlt = ((_ii[:, None] < _ii[None, :]) & ((_ii[:, None] // 64) == (_ii[None, :] // 64))).astype(np.float32)
    moff = ((_ii[:, None] < 64) & (_ii[None, :] >= 64)).astype(np.float32)
    bc = lambda v: np.ascontiguousarray(np.broadcast_to(v, (128,) + v.shape))
    shared = dict(gA=_gT([norm_g[0, 0]]),
                  gT0=_gT([norm_g[0, 1], norm_g[0, 2], norm_g[0, 3], norm_g[0, 4], norm_g[0, 5], norm_g[1, 0], mem_norm_g]),
                  gT1=_gT([norm_g[1, 1], norm_g[1, 2], norm_g[1, 3], norm_g[1, 4], norm_g[1, 5], norm_g[0, 0], mem_norm_g]),
                  identb=eye.astype(bf), identf=eye, tri=tri, nmle=nmle, mlt=mlt, moff=moff,
                  w_out0=f(mlstm_w_out)[0], w_out1=f(gdn_w_out)[0], w_xq0=f(w_xq)[0], w_xq1=f(w_xq)[1], w_xo0=f(w_xo)[0], w_xo1=f(w_xo)[1],
                  w_up0=f(w_up)[0], w_up1=f(w_up)[1], w_down0=f(w_down)[0], w_down1=f(w_down)[1], w_kv=f(w_mem_kv), gd_ng=bc(f(gdn_norm_g)[0]))
    Wm = f(mlstm_w_in)[0]; bgs = f(mlstm_b_gates)[0]; hgs = f(mlstm_head_g)[0]
    Wg = f(gdn_w_in)[0]; CW = f(gdn_conv_w)[0]
    memT = [np.ascontiguousarray(mem[b].T) for b in range(B)]
    ins = []
    for c in cores:
        b, r = c // 4, c % 4
        d = dict(shared)
        d["xT"] = np.ascontiguousarray(x[b, r * NT:(r + 1) * NT].T)
        d["memT"] = memT[b]
        d["sel"] = bc(np.eye(4, dtype=np.float32)[r])
        wq = Wm[:, r * 256:(r + 1) * 256]; wk = Wm[:, 1024 + r * 256:1024 + (r + 1) * 256]
        wv = Wm[:, 2048 + r * 512:2048 + (r + 1) * 512]; wo = Wm[:, 4096 + r * 512:4096 + (r + 1) * 512]
        wg = Wm[:, [6144 + r, 6148 + r]]
        d["ml_w_fm"] = np.ascontiguousarray(np.concatenate([wq, wk], 1))
        d["ml_w_tm"] = np.ascontiguousarray(np.concatenate([wv, wo, wk, wg], 1))
        d["ml_bg"] = bc(np.array([bgs[r], bgs[4 + r]], np.float32))
        d["ml_hg"] = bc(hgs[r * 512:(r + 1) * 512])
        cq = np.arange(r * 512, (r + 1) * 512); ck = 2048 + cq; cv = 4096 + np.arange(r * 1024, (r + 1) * 1024)
        cc = np.concatenate([cq, ck, cv])
        cz = 8192 + np.arange(r * 1024, (r + 1) * 1024); cb = 12288 + np.arange(r * 8, (r + 1) * 8); ca = 12320 + np.arange(r * 8, (r + 1) * 8)
        d["gd_w_fm"] = np.ascontiguousarray(Wg[:, cc])
        d["gd_w_tm"] = np.ascontiguousarray(Wg[:, np.concatenate([cz, cb, ca])])
        d["gd_cw"] = np.ascontiguousarray(CW[:, cc].reshape(4, 16, 128).transpose(2, 1, 0))
        d["gd_alog"] = bc(f(gdn_a_log)[0][r * 8:(r + 1) * 8])
        d["gd_dtb"] = bc(f(gdn_dt_bias)[0][r * 8:(r + 1) * 8])
        ins.append(d)
    res = run_bass_kernel_spmd(build_fused(S), ins, core_ids=cores)
    out = np.empty((B, S, D), np.float32)
    for c in cores:
        out[c // 4, (c % 4) * NT:(c % 4 + 1) * NT] = res.results[c]["xo"].T
    return out
```
